# Optimizing a Trainium2 kernel written in Bass

```python
import jax, jax.numpy as jnp
from jax import lax
import numpy as np

D_MODEL = 1024
BATCH = 4
SEQ = 8192
DEPTH = 1

HEAD_DIM = 64
N_Q_HEADS = 8
N_KV_HEADS = 2
WINDOW = 128
ATTN_BLOCK = 128
ROPE_THETA = 10000.0
M_HEADS = 4
M_QK_DIM = 64
M_V_DIM = 128
M_CHUNK = 128
CONV_WIDTH = 4
D_FF = -(-8 * D_MODEL // (3 * 256)) * 256
N_BRANCH = 2
EPS = 1e-6

ATTN_Q_W = N_Q_HEADS * HEAD_DIM
ATTN_KV_W = N_KV_HEADS * HEAD_DIM
M_QK_W = M_HEADS * M_QK_DIM
M_V_W = M_HEADS * M_V_DIM
IN_WIDTHS = (ATTN_Q_W, ATTN_KV_W, ATTN_KV_W, M_QK_W, M_QK_W, M_V_W, M_V_W, M_HEADS, M_HEADS, N_BRANCH * D_MODEL)
IN_WIDTH = sum(IN_WIDTHS)

kernel_name = 'hybrid_swa_sink_mlstm_gated_sandwich'


def rmsnorm(x, g):
    xf = x.astype(jnp.float32)
    y = xf * lax.rsqrt(jnp.mean(xf * xf, axis=-1, keepdims=True) + EPS)
    return y.astype(x.dtype) * g


def rope(x, pos):
    inv_freq = ROPE_THETA ** (-jnp.arange(0, HEAD_DIM, 2, dtype=jnp.float32) / HEAD_DIM)
    ang = pos.astype(jnp.float32)[:, None] * inv_freq[None, :]
    emb = jnp.concatenate([ang, ang], axis=-1)
    cos = jnp.cos(emb)[None, :, None, :].astype(x.dtype)
    sin = jnp.sin(emb)[None, :, None, :].astype(x.dtype)
    x1, x2 = jnp.split(x, 2, axis=-1)
    rot = jnp.concatenate([-x2, x1], axis=-1)
    return x * cos + rot * sin


def sliding_window_attention(q, k, v, sinks):
    B, S = q.shape[0], q.shape[1]
    blk = ATTN_BLOCK
    nb = S // blk
    G = N_Q_HEADS // N_KV_HEADS
    qb = q.reshape(B, nb, blk, N_KV_HEADS, G, HEAD_DIM)
    kb = k.reshape(B, nb, blk, N_KV_HEADS, HEAD_DIM)
    vb = v.reshape(B, nb, blk, N_KV_HEADS, HEAD_DIM)

    def with_prev(t):
        prev = jnp.pad(t, ((0, 0), (1, 0), (0, 0), (0, 0), (0, 0)))[:, :-1]
        return jnp.concatenate([prev, t], axis=2)

    kw = with_prev(kb)
    vw = with_prev(vb)
    scores = jnp.einsum('bnqhgd,bnkhd->bnhgqk', qb, kw).astype(jnp.float32) * (HEAD_DIM ** -0.5)
    qi = jnp.arange(blk)[:, None]
    kj = jnp.arange(2 * blk)[None, :]
    rel = qi + blk - kj
    band = (rel >= 0) & (rel < WINDOW)
    not_before_start = (jnp.arange(nb)[:, None, None] > 0) | (kj >= blk)[None]
    mask = band[None] & not_before_start
    scores = jnp.where(mask[None, :, None, None], scores, -jnp.inf)
    sink = jnp.broadcast_to(sinks.astype(jnp.float32).reshape(1, 1, N_KV_HEADS, G, 1, 1), scores.shape[:-1] + (1,))
    probs = jax.nn.softmax(jnp.concatenate([scores, sink], axis=-1), axis=-1)[..., :-1]
    out = jnp.einsum('bnhgqk,bnkhd->bnqhgd', probs.astype(v.dtype), vw)
    return out.reshape(B, S, N_Q_HEADS * HEAD_DIM)


def causal_depthwise_conv(x, w, b):
    C = x.shape[-1]
    y = lax.conv_general_dilated(x, w[:, None, :], window_strides=(1,), padding=[(CONV_WIDTH - 1, 0)],
                                 dimension_numbers=('NWC', 'WIO', 'NWC'), feature_group_count=C)
    return y + b


def mlstm_chunkwise(q, k, v, i_pre, f_pre):
    B, S = q.shape[0], q.shape[1]
    L = M_CHUNK
    nc = S // L

    def heads_first(t):
        t = t.reshape((B, nc, L) + t.shape[2:])
        return jnp.moveaxis(t, 3, 1)

    qc = heads_first(q)
    kc = heads_first(k) * (M_QK_DIM ** -0.5)
    vc = heads_first(v)
    ig = heads_first(i_pre.astype(jnp.float32))
    log_f = jax.nn.log_sigmoid(heads_first(f_pre.astype(jnp.float32)))
    b = jnp.cumsum(log_f, axis=-1)
    g = b[..., -1]
    causal = jnp.tril(jnp.ones((L, L), dtype=bool))
    log_d = jnp.where(causal, b[..., :, None] - b[..., None, :] + ig[..., None, :], -jnp.inf)

    log_end = g[..., None] - b + ig
    m_loc = jnp.max(log_end, axis=-1)
    kw = kc * jnp.exp(log_end - m_loc[..., None])[..., None]
    a_c = jnp.einsum('bhcsd,bhcse->bhcde', kw, vc)
    n_c = jnp.sum(kw, axis=-2)

    def step(carry, xs):
        C, n, m = carry
        a_i, n_i, g_i, ml_i = xs
        m_new = jnp.maximum(g_i + m, ml_i)
        decay = jnp.exp(g_i + m - m_new)
        scale = jnp.exp(ml_i - m_new)
        C_new = decay[..., None, None] * C + scale[..., None, None] * a_i
        n_new = decay[..., None] * n + scale[..., None] * n_i
        return (C_new, n_new, m_new), (C, n, m)

    init = (jnp.zeros((B, M_HEADS, M_QK_DIM, M_V_DIM), jnp.float32),
            jnp.zeros((B, M_HEADS, M_QK_DIM), jnp.float32),
            jnp.zeros((B, M_HEADS), jnp.float32))
    xs = (jnp.moveaxis(a_c, 2, 0), jnp.moveaxis(n_c, 2, 0), jnp.moveaxis(g, 2, 0), jnp.moveaxis(m_loc, 2, 0))
    _, (c_prev, n_prev, m_prev) = lax.scan(step, init, xs)
    c_prev = jnp.moveaxis(c_prev, 0, 2)
    n_prev = jnp.moveaxis(n_prev, 0, 2)
    m_prev = jnp.moveaxis(m_prev, 0, 2)

    log_inter = b + m_prev[..., None]
    m_t = jnp.maximum(log_inter, jnp.max(log_d, axis=-1))
    s_mat = jnp.einsum('bhctd,bhcsd->bhcts', qc, kc) * jnp.exp(log_d - m_t[..., None])
    inter = jnp.exp(log_inter - m_t)
    num = jnp.einsum('bhcts,bhcse->bhcte', s_mat, vc) + inter[..., None] * jnp.einsum('bhctd,bhcde->bhcte', qc, c_prev)
    den = jnp.sum(s_mat, axis=-1) + inter * jnp.einsum('bhctd,bhcd->bhct', qc, n_prev)
    h = num / jnp.maximum(jnp.abs(den), jnp.exp(-m_t))[..., None]
    return jnp.moveaxis(h, 1, 3).reshape(B, S, M_HEADS, M_V_DIM)


def head_rmsnorm(h, g):
    hf = h.astype(jnp.float32)
    y = hf * lax.rsqrt(jnp.mean(hf * hf, axis=-1, keepdims=True) + EPS)
    return y * g


def setup_inputs(seed: int = 0) -> dict:
    key = jax.random.key(seed)
    ks = jax.random.split(key, 18)
    f32 = jnp.float32

    def nrm(k, shape, scale):
        return jax.random.normal(k, shape, f32) * scale

    def gain(k, n):
        return 1.0 + nrm(k, (DEPTH, n), 0.02)

    b_fgate = jnp.linspace(3.0, 6.0, M_HEADS, dtype=f32)[None, :] + nrm(ks[10], (DEPTH, M_HEADS), 0.1)
    return {
        'x': nrm(ks[0], (BATCH, SEQ, D_MODEL), 1.0),
        'norm_pre_mix': gain(ks[1], D_MODEL),
        'norm_post_mix': gain(ks[2], D_MODEL),
        'norm_pre_ffn': gain(ks[3], D_MODEL),
        'norm_post_ffn': gain(ks[4], D_MODEL),
        'w_in': nrm(ks[5], (DEPTH, D_MODEL, IN_WIDTH), D_MODEL ** -0.5),
        'attn_sinks': nrm(ks[6], (DEPTH, N_Q_HEADS), 0.5),
        'conv_w': nrm(ks[7], (DEPTH, CONV_WIDTH, 2 * M_QK_W), CONV_WIDTH ** -0.5),
        'conv_b': nrm(ks[8], (DEPTH, 2 * M_QK_W), 0.02),
        'b_igate': nrm(ks[9], (DEPTH, M_HEADS), 0.1),
        'b_fgate': b_fgate,
        'mlstm_head_norm': gain(ks[11], M_V_W),
        'w_attn_branch': nrm(ks[12], (DEPTH, ATTN_Q_W, D_MODEL), ATTN_Q_W ** -0.5),
        'w_mlstm_branch': nrm(ks[13], (DEPTH, M_V_W, D_MODEL), M_V_W ** -0.5),
        'w_out': nrm(ks[14], (DEPTH, D_MODEL, D_MODEL), D_MODEL ** -0.5),
        'w_ffn_in': nrm(ks[15], (DEPTH, D_MODEL, 2 * D_FF), D_MODEL ** -0.5),
        'w_ffn_out': nrm(ks[16], (DEPTH, D_FF, D_MODEL), D_FF ** -0.5),
    }


def reference(x, norm_pre_mix, norm_post_mix, norm_pre_ffn, norm_post_ffn, w_in, attn_sinks, conv_w, conv_b,
              b_igate, b_fgate, mlstm_head_norm, w_attn_branch, w_mlstm_branch, w_out, w_ffn_in, w_ffn_out):
    B, S, _ = x.shape
    pos = jnp.arange(S)
    split_points = np.cumsum(IN_WIDTHS)[:-1].tolist()
    for l in range(DEPTH):
        h = rmsnorm(x, norm_pre_mix[l])
        proj = h @ w_in[l]
        aq, ak, av, mq, mk, mv, mo, mi, mf, gates = jnp.split(proj, split_points, axis=-1)

        aq = rope(aq.reshape(B, S, N_Q_HEADS, HEAD_DIM), pos)
        ak = rope(ak.reshape(B, S, N_KV_HEADS, HEAD_DIM), pos)
        av = av.reshape(B, S, N_KV_HEADS, HEAD_DIM)
        attn_out = sliding_window_attention(aq, ak, av, attn_sinks[l])

        mqk = jax.nn.silu(causal_depthwise_conv(jnp.concatenate([mq, mk], axis=-1), conv_w[l], conv_b[l]))
        mq, mk = jnp.split(mqk, 2, axis=-1)
        cell = mlstm_chunkwise(mq.reshape(B, S, M_HEADS, M_QK_DIM), mk.reshape(B, S, M_HEADS, M_QK_DIM),
                               mv.reshape(B, S, M_HEADS, M_V_DIM), mi + b_igate[l], mf + b_fgate[l])
        cell = head_rmsnorm(cell, mlstm_head_norm[l].reshape(M_HEADS, M_V_DIM)).astype(x.dtype)
        mlstm_out = jax.nn.sigmoid(mo) * cell.reshape(B, S, M_V_W)

        g_attn, g_mlstm = jnp.split(jax.nn.sigmoid(gates), N_BRANCH, axis=-1)
        merged = g_attn * (attn_out @ w_attn_branch[l]) + g_mlstm * (mlstm_out @ w_mlstm_branch[l])
        x = x + rmsnorm(merged @ w_out[l], norm_post_mix[l])

        h2 = rmsnorm(x, norm_pre_ffn[l])
        gate, up = jnp.split(h2 @ w_ffn_in[l], 2, axis=-1)
        y = (jax.nn.silu(gate) * up) @ w_ffn_out[l]
        x = x + rmsnorm(y, norm_post_ffn[l])
    return x
```

```python
import numpy as np
import ml_dtypes
from contextlib import ExitStack
import concourse.bass as bass
import concourse.mybir as mybir
from concourse.bass_utils import run_bass_kernel_spmd
from concourse.ap import AP

F32 = mybir.dt.float32
BF16 = mybir.dt.bfloat16
ALU = mybir.AluOpType
AF = mybir.ActivationFunctionType
AX = mybir.AxisListType

NCORES = 8
D = 1024
HALF = 4096
NCH = 32
T1 = 256
CPT = T1 // 128
NT1 = HALF // T1
T2 = 512
NT2 = HALF // T2
DFF = 2816
NJB = DFF // 128
INW = 4360
EPS = 1e-6
C_AQ, C_AK, C_AV, C_MQ, C_MK, C_MV, C_MO, C_MI, C_MF, C_GA, C_GM = 0, 512, 640, 768, 1024, 1280, 1792, 2304, 2308, 2312, 3336


class Sched:
    ENG = ("pe", "dve", "act", "pool", "sp")

    def __init__(self, nc, es, n_dma=16):
        self.nc = nc
        self.semobj = {}
        self.cnt = {}
        self.base = {}
        for n in self.ENG:
            self.semobj[("e", n)] = es.enter_context(nc.semaphore("s_" + n))
            self.cnt[("e", n)] = 0
            self.base[("e", n)] = 0
        self.ndma = n_dma
        for i in range(n_dma):
            self.semobj[("d", i)] = es.enter_context(nc.semaphore("d%d" % i))
            self.cnt[("d", i)] = 0
        self.dnext2 = [0, 0]
        self.seen = {n: {} for n in self.ENG}
        self.lastw = {}
        self.readers = {}
        self.prog = {n: [] for n in self.ENG}

    def _wait(self, eng, sid, val):
        if val <= 0:
            return
        s = self.seen[eng]
        if s.get(sid, 0) >= val:
            return
        s[sid] = val
        self.prog[eng].append(("wait", sid, val))

    def _need(self, eng, reads, writes, is_dma):
        me = ("e", eng)
        need = {}

        def add(sid, v):
            if need.get(sid, 0) < v:
                need[sid] = v

        for k in reads:
            lw = self.lastw.get(k)
            if lw is not None:
                if lw[0] == me and eng == "pe" and not is_dma:
                    continue
                add(*lw)
        strict = is_dma or eng != "pe"
        for k in writes:
            lw = self.lastw.get(k)
            if lw is not None and (lw[0] != me or strict):
                add(*lw)
            for sid, v in self.readers.get(k, {}).items():
                if sid != me or strict:
                    add(sid, v)
        return need

    def op(self, eng, fn, reads=(), writes=()):
        if eng != "pe":
            extra = [("rd", k) for k in reads if k == "PT" or (isinstance(k, tuple) and k[0] in ("PS", "PY"))]
            if extra:
                writes = list(writes) + extra
        need = self._need(eng, reads, writes, False)
        for sid, v in need.items():
            self._wait(eng, sid, v)
        me = ("e", eng)
        self.cnt[me] += 1
        c = self.cnt[me]
        self.prog[eng].append(("op", fn, c))
        for k in reads:
            self.readers.setdefault(k, {})[me] = c
        for k in writes:
            self.lastw[k] = (me, c)
            self.readers[k] = {}

    def dma(self, q, out, in_, reads=(), writes=()):
        half = self.ndma // 2
        qi = 1 if q == "pool" else 0
        i = self.dnext2[qi] + qi * half
        self.dnext2[qi] = (self.dnext2[qi] + 1) % half
        sid = ("d", i)
        need = self._need(q, reads, writes, True)
        need[sid] = max(need.get(sid, 0), self.cnt[sid])
        for s, v in need.items():
            self._wait(q, s, v)
        self.cnt[sid] += 16
        c = self.cnt[sid]
        self.prog[q].append(("dma", out, in_, sid))
        for k in reads:
            self.readers.setdefault(k, {})[sid] = c
        for k in writes:
            self.lastw[k] = (sid, c)
            self.readers[k] = {}

    def barrier(self):
        for eng in self.ENG:
            for sid, v in self.cnt.items():
                if sid == ("e", eng):
                    continue
                self._wait(eng, sid, v)

    def emit(self):
        nc = self.nc
        prog = self.prog
        self.prog = {n: [] for n in self.ENG}
        waited = {("e", n): set() for n in self.ENG}
        for eng in self.ENG:
            for ent in prog[eng]:
                if ent[0] == "wait" and ent[1][0] == "e":
                    waited[ent[1]].add(ent[2])
        remap = {}
        for sid, vs in waited.items():
            b0 = self.base[sid]
            remap[sid] = {v: b0 + i + 1 for i, v in enumerate(sorted(vs))}
            self.base[sid] = b0 + len(vs)
        semobj = self.semobj
        self.n_inc = {n: len(waited[("e", n)]) for n in self.ENG}

        def run(eng, e):
            me = ("e", eng)
            for ent in prog[eng]:
                if ent[0] == "wait":
                    _, sid, v = ent
                    e.wait_ge(semobj[sid], remap[sid][v] if sid[0] == "e" else v)
                elif ent[0] == "op":
                    ins = ent[1](e)
                    if ent[2] in remap[me]:
                        ins.then_inc(semobj[me], 1)
                else:
                    _, out, in_, sid = ent
                    e.dma_start(out=out, in_=in_).then_inc(semobj[sid], 16)

        with nc.Block() as block:
            @block.tensor
            def _(e):
                run("pe", e)

            @block.vector
            def _(e):
                run("dve", e)

            @block.scalar
            def _(e):
                run("act", e)

            @block.gpsimd
            def _(e):
                run("pool", e)

            @block.sync
            def _(e):
                run("sp", e)


def bc(a, pos, n):
    dims = [list(d) for d in a.ap]
    dims.insert(pos, [0, n])
    return AP(a.tensor, a.offset, dims)


def build_program():
    nc = bass.Bass("TRN2", target_bir_lowering=False)

    def din(name, shape, dt=F32):
        return nc.dram_tensor(name, list(shape), dt, kind="ExternalInput").ap()

    xm = din("xm", [HALF, D])
    xp = din("xp", [HALF, D])
    w_in = din("w_in", [D, INW])
    w_ab = din("w_ab", [512, D])
    w_mb = din("w_mb", [512, D])
    w_o = din("w_o", [D, D])
    w_fi = din("w_fi", [D, 2 * DFF])
    w_fo = din("w_fo", [DFF, D])
    gains_d = din("gains", [128, 4, D])
    hng_d = din("hng", [128, 512])
    sinks_d = din("sinks", [128, 8])
    bif_d = din("bif", [128, 8])
    convw_d = din("convw", [128, 4, 4])
    convb_d = din("convb", [128, 4])
    flag_d = din("flag", [128, 1])
    cos_d = din("cosT", [128, NCH + 1, 32])
    sin_d = din("sinS", [128, NCH + 1, 64])
    mask_d = din("mask", [128, 2, 256], BF16)
    idb_d = din("idb", [128, 128], BF16)
    idf_d = din("idf", [128, 128])
    trif_d = din("trif", [128, 128])
    trib_d = din("trib", [128, 128], BF16)
    onesf_d = din("onesf", [128, 128])
    out = nc.dram_tensor("out", [HALF, D], F32, kind="ExternalOutput").ap()

    with ExitStack() as es:
        S = Sched(nc, es)
        PT = es.enter_context(nc.psum_tensor("PT", [128, 1024], BF16))
        PY = [es.enter_context(nc.psum_tensor("PY%d" % i, [128, 1024], F32)) for i in range(2)]
        PS = [es.enter_context(nc.psum_tensor("PS%d" % i, [128, 512], F32)) for i in range(3)]
        rr = [0]

        def nextP():
            i = rr[0]
            rr[0] = (i + 1) % 3
            return PS[i], ("PS", i)

        with ExitStack() as e1:
            def sb(name, shape, dt):
                return e1.enter_context(nc.sbuf_tensor(name, list(shape), dt))

            Wi = sb("Wi", [128, 8, INW], BF16)
            Wab = sb("Wab", [128, 4, D], BF16)
            Wmb = sb("Wmb", [128, 4, D], BF16)
            Wo = sb("Wo", [128, 8, D], BF16)
            cosT = sb("cosT_s", [128, NCH + 1, 32], F32)
            sinS = sb("sinS_s", [128, NCH + 1, 64], F32)
            gains = sb("gains_s", [128, 2, D], F32)
            hng = sb("hng_s", [128, 512], F32)
            sinks = sb("sinks_s", [128, 8], F32)
            bif = sb("bif_s", [128, 8], F32)
            convw = sb("convw_s", [128, 4, 4], F32)
            convb = sb("convb_s", [128, 4], F32)
            flag = sb("flag_s", [128, 1], F32)
            mask = sb("mask_s", [128, 2, 256], BF16)
            idb = sb("idb_s", [128, 128], BF16)
            idf = sb("idf_s", [128, 128], F32)
            trif = sb("trif_s", [128, 128], F32)
            trib = sb("trib_s", [128, 128], BF16)
            onesf = sb("onesf_s", [128, 128], F32)

            x_sb = [sb("x_sb%d" % i, [128, CPT, D], F32) for i in range(2)]
            h_sb = sb("h_sb", [128, D], BF16)
            junk = sb("junk", [128, D], BF16)
            hT = sb("hT", [128, 8, T1], BF16)
            scr = sb("scr", [128, 1280], F32)
            qk_r = sb("qk_r", [128, 640], BF16)
            qT = sb("qT", [128, 4, T1], BF16)
            kTb = sb("kTb", [128, (CPT + 1) * 128], BF16)
            vb = sb("vb", [128, CPT + 1, 2, 66], BF16)
            mv_aug = sb("mv_aug", [128, CPT, 4, 130], BF16)
            tmo = sb("tmo", [128, CPT, 512], BF16)
            gates_tm = sb("gates_tm", [128, CPT, 8], F32)
            conv_buf = sb("conv_buf", [128, 4, T1 + 3], F32)
            cacc = sb("cacc", [128, 4, T1], F32)
            stg = sb("stg", [128, 2, 1024], F32)
            ctanh = sb("ctanh", [128, 4, T1], BF16)
            mqkT = sb("mqkT", [128, 4, T1], BF16)
            k_tm = sb("k_tm", [128, CPT, 256], BF16)
            pexp = sb("pexp", [128, 4, 256], BF16)
            pTt = sb("pTt", [128, 2, 4, 128], BF16)
            attn_o = sb("attn_o", [128, 512], BF16)
            attn_oT = sb("attn_oT", [128, 4, T1], BF16)
            kw = sb("kw", [128, 4, 64], BF16)
            STw = sb("STw", [128, 4, 128], BF16)
            cellg = sb("cellg", [128, 4, 128], BF16)
            mo_out = sb("mo_out", [128, 512], BF16)
            mlstm_oT = sb("mlstm_oT", [128, 4, T1], BF16)
            Cst = sb("Cst", [128, 2, 130], F32)
            Chat = sb("Chat", [128, 2, 130], F32)
            Chat_z = sb("Chat_z", [128, 4, 130], BF16)
            tg = sb("tg", [128, 512], BF16)
            mergedT = sb("mergedT", [128, 8, T1], BF16)
            st = sb("st", [128, 64], F32)
            lfp = sb("lfp", [128, CPT, 4], F32)
            gb = sb("gb", [128, CPT, 8], F32)
            ab = sb("ab", [128, 2, CPT * 4], F32)
            d2 = sb("d2", [128, 2, CPT * 4], F32)
            ew = sb("ew", [128, 2, CPT * 4], F32)
            wq = sb("wq", [128, CPT * 4], F32)
            fbc = sb("fbc", [128, CPT * 4], F32)
            hp = sb("hp", [4, 32], F32)
            Rm = sb("Rm", [128, 2 * CPT, 4], F32)
            mst = sb("mst", [4, 1], F32)

            for dst, src, k in [(cosT, cos_d, "cos"), (sinS, sin_d, "sin"), (hng, hng_d, "hng"), (sinks, sinks_d, "sinks"),
                                (bif, bif_d, "bif"), (convw, convw_d, "convw"), (convb, convb_d, "convb"), (flag, flag_d, "flag"),
                                (mask, mask_d, "mask"), (idb, idb_d, "idb"), (idf, idf_d, "idf"), (trif, trif_d, "trif"),
                                (trib, trib_d, "trib"), (onesf, onesf_d, "onesf")]:
                S.dma("sp", dst[:], src, writes=[k])
            S.dma("sp", gains[:], gains_d[:, 0:2, :], writes=["gains"])

            stg_i = [0]
            pending = []

            def load_cast(dst, src, wkeys_, scale=1.0):
                def go():
                    sl = stg_i[0] % 2
                    stg_i[0] += 1
                    shp = list(dst.shape)
                    n = 1
                    for d_ in shp[1:]:
                        n *= d_
                    v = stg[:, sl, 0:n]
                    if len(shp) == 3:
                        v = v.rearrange("p (a b) -> p a b", a=shp[1])
                    S.dma("sp", v, src, writes=[("stg", sl)])
                    S.op("act", lambda e: e.activation(out=dst, in_=v, func=AF.Copy, scale=scale), reads=[("stg", sl)], writes=wkeys_)
                pending.append(go)

            def flush(n):
                for _ in range(min(n, len(pending))):
                    pending.pop(0)()

            w_in_v = w_in.rearrange("(kc p) n -> p kc n", p=128)
            wranges = [(C_MK, C_MV), (C_MV, C_MO), (C_MI, C_GA), (C_AK, C_MQ), (C_AQ, C_AK), (C_MO, C_MI), (C_MQ, C_MK),
                       (C_GA, C_GA + 512), (C_GA + 512, C_GM), (C_GM, C_GM + 512), (C_GM + 512, INW)]
            wkeys = {}
            for (a, b) in wranges:
                for c in range(a, b):
                    wkeys[c] = ("Wi", a)

            def load_w_in(rs):
                for (a, b) in rs:
                    g = max(1, min(4, 1024 // (b - a)))
                    for k0 in range(0, 8, g):
                        load_cast(Wi[:, k0:k0 + g, a:b], w_in_v[:, k0:k0 + g, a:b], [("Wi", a, k0 // 4)])

            def wk(col):
                k = wkeys[col]
                return [(k[0], k[1], 0), (k[0], k[1], 1)]

            load_w_in(wranges[:3])
            flush(len(pending))
            load_w_in(wranges[3:])
            for (dst, src, k) in [(Wab, w_ab, "Wab"), (Wmb, w_mb, "Wmb")]:
                sv = src.rearrange("(kc p) n -> p kc n", p=128)
                for kc in range(4):
                    load_cast(dst[:, kc, :], sv[:, kc, :], [k], scale=0.5)
            sv = w_o.rearrange("(kc p) n -> p kc n", p=128)
            for kc in range(8):
                load_cast(Wo[:, kc, :], sv[:, kc, :], ["Wo"])

            S.op("dve", lambda e: e.memset(Cst[:], 0.0), writes=["Cst"])
            S.op("dve", lambda e: e.memset(mst[:], 0.0), writes=["mst"])
            S.op("dve", lambda e: e.memset(Chat_z[:], 0.0), writes=["Chat_bf"])
            S.op("dve", lambda e: e.memset(Rm[:], 0.0), writes=["Rm"])
            S.op("dve", lambda e: e.memset(conv_buf[:], 0.0), writes=["convbuf"])
            S.op("dve", lambda e: e.memset(mv_aug[:, :, :, 128:130], 1.0), writes=["mv_aug0", "mv_aug1"])
            S.op("dve", lambda e: e.memset(vb[:, :, :, 64:66], 1.0), writes=["vb0", "vb1", "vb2"])
            S.op("dve", lambda e: e.memset(kTb[:, 0:128], 0.0), writes=["kT0"])
            S.op("dve", lambda e: e.memset(vb[:, 0, :, 0:64], 0.0), reads=["vb0"], writes=["vb0"])
            S.op("dve", lambda e: e.tensor_scalar(out=convw[:], in0=convw[:], scalar1=0.5, scalar2=None, op0=ALU.mult), reads=["convw"], writes=["convw"])
            S.op("dve", lambda e: e.tensor_scalar(out=convb[:], in0=convb[:], scalar1=0.5, scalar2=None, op0=ALU.mult), reads=["convb"], writes=["convb"])
            S.op("dve", lambda e: e.tensor_scalar(out=hng[:], in0=hng[:], scalar1=0.5, scalar2=None, op0=ALU.mult), reads=["hng"], writes=["hng"])

            def load_x(src, ti):
                slot = ti % 2
                for c in range(CPT):
                    r0 = (ti * CPT + c) * 128
                    S.dma("sp", x_sb[slot][:, c, :], src[r0:r0 + 128, :], writes=[("x", slot, c)])

            def rstd_from_ssq(col, n, inv_n, key):
                v = st[:, col:col + n]
                S.op("dve", lambda e: e.tensor_scalar(out=v, in0=v, scalar1=inv_n, scalar2=EPS, op0=ALU.mult, op1=ALU.add), reads=[key], writes=[key])
                S.op("act", lambda e: e.activation(out=v, in_=v, func=AF.Ln), reads=[key], writes=[key])
                S.op("act", lambda e: e.activation(out=v, in_=v, func=AF.Exp, scale=-0.5), reads=[key], writes=[key])

            def transpose_to(dst_ap, srcs, reads, writes, evac="act"):
                n = len(srcs)
                for i, s_ap in enumerate(srcs):
                    S.op("pe", lambda e, s_ap=s_ap, i=i: e.transpose(PT[:, i * 128:(i + 1) * 128], s_ap, idb[:]),
                         reads=list(reads) + ["idb"], writes=["PT"])
                src = PT[:, 0:n * 128]
                if evac == "act":
                    S.op("act", lambda e: e.activation(out=dst_ap, in_=src if len(dst_ap.shape) == 2 else src.rearrange("p (a b) -> p a b", b=128), func=AF.Copy),
                         reads=["PT"], writes=writes)
                else:
                    S.op("dve", lambda e: e.tensor_copy(out=dst_ap, in_=src if len(dst_ap.shape) == 2 else src.rearrange("p (a b) -> p a b", b=128)),
                         reads=["PT"], writes=writes)

            def mm_group(out_ap, pkey, pairs, reads):
                n = len(pairs)
                for i, (l, r) in enumerate(pairs):
                    S.op("pe", lambda e, l=l, r=r, i=i: e.matmul(out_ap, lhsT=l, rhs=r, start=(i == 0), stop=(i == n - 1)),
                         reads=reads, writes=[pkey])

            def mix_tile(ti, pre, last_pre):
                slot = ti % 2
                xs = x_sb[slot]
                main = not pre
                need_kv = main or last_pre
                S.op("dve", lambda e: e.memset(st[:, 0:8], 0.0), writes=["st_n"])
                for c in range(CPT):
                    S.op("act", lambda e, c=c: e.activation(out=junk[:], in_=xs[:, c, :], func=AF.Square, accum_out=st[:, c:c + 1]),
                         reads=[("x", slot, c), "st_n"], writes=["junk", "st_n"])
                rstd_from_ssq(0, CPT, 1.0 / D, "st_n")
                for c in range(CPT):
                    S.op("dve", lambda e, c=c: e.scalar_tensor_tensor(out=h_sb[:], in0=xs[:, c, :], scalar=st[:, c:c + 1], in1=gains[:, 0, :],
                                                                      op0=ALU.mult, op1=ALU.mult),
                         reads=[("x", slot, c), "st_n", "gains"], writes=["h_sb"])
                    transpose_to(hT[:, :, c * 128:(c + 1) * 128], [h_sb[:, k * 128:(k + 1) * 128] for k in range(8)],
                                 reads=["h_sb"], writes=[("hT", c)])
                hTk = [("hT", c) for c in range(CPT)]

                for c in range(CPT):
                    j = ti * CPT + c + 1
                    lhs = [hT[:, k, c * 128:(c + 1) * 128] for k in range(8)]
                    if need_kv:
                        tj = j if main else 0
                        store_kv = main or (c == CPT - 1)
                    if main:
                        P, pk = nextP()
                        mm_group(P[:, 0:512], pk, [(lhs[k], Wi[:, k, C_AQ:C_AQ + 512]) for k in range(8)], reads=[("hT", c)] + wk(C_AQ))
                        qv = P[:, 0:512].rearrange("p (h t d) -> p h t d", h=8, t=2)
                        t1 = scr[:, 0:512].rearrange("p (h t d) -> p h t d", h=8, t=2)
                        t2 = scr[:, 640:1152].rearrange("p (h t d) -> p h t d", h=8, t=2)
                        cs = bc(cosT[:, tj, :], 1, 2)
                        sn = sinS[:, tj, :].rearrange("p (t d) -> p t d", t=2)
                        S.op("dve", lambda e, qv=qv, t1=t1, cs=cs: e.tensor_tensor(out=t1, in0=qv, in1=bc(cs, 1, 8), op=ALU.mult),
                             reads=[pk, "cos"], writes=["scr_a"])
                        S.op("dve", lambda e, qv=qv, t2=t2, sn=sn: e.tensor_tensor(out=t2[:, :, 0, :], in0=qv[:, :, 1, :], in1=bc(sn[:, 0, :], 1, 8), op=ALU.mult),
                             reads=[pk, "sin"], writes=["scr_b"])
                        S.op("dve", lambda e, qv=qv, t2=t2, sn=sn: e.tensor_tensor(out=t2[:, :, 1, :], in0=qv[:, :, 0, :], in1=bc(sn[:, 1, :], 1, 8), op=ALU.mult),
                             reads=[pk, "sin"], writes=["scr_b"])
                        S.op("dve", lambda e: e.tensor_tensor(out=qk_r[:, 0:512].rearrange("p (g kv d) -> p kv g d", g=4, kv=2),
                                                               in0=scr[:, 0:512].rearrange("p (kv g d) -> p kv g d", kv=2, g=4),
                                                               in1=scr[:, 640:1152].rearrange("p (kv g d) -> p kv g d", kv=2, g=4), op=ALU.add),
                             reads=["scr_a", "scr_b"], writes=["qk_r"])
                    P, pk = nextP()
                    if need_kv:
                        mm_group(P[:, 0:256], pk, [(lhs[k], Wi[:, k, C_AK:C_AK + 256]) for k in range(8)], reads=[("hT", c)] + wk(C_AK))
                    mm_group(P[:, 256:264], pk, [(lhs[k], Wi[:, k, C_MI:C_MI + 8]) for k in range(8)], reads=[("hT", c)] + wk(C_MI))
                    S.op("act", lambda e, P=P, c=c: e.activation(out=gates_tm[:, c, :], in_=P[:, 256:264], func=AF.Copy), reads=[pk], writes=["gates_tm"])
                    if need_kv and store_kv:
                        slotk = c + 1 if main else 0
                        kv_ = P[:, 0:128].rearrange("p (h t d) -> p h t d", h=2, t=2)
                        t1 = scr[:, 512:640].rearrange("p (h t d) -> p h t d", h=2, t=2)
                        t2 = scr[:, 1152:1280].rearrange("p (h t d) -> p h t d", h=2, t=2)
                        cs = bc(cosT[:, tj, :], 1, 2)
                        sn = sinS[:, tj, :].rearrange("p (t d) -> p t d", t=2)
                        S.op("dve", lambda e, kv_=kv_, t1=t1, cs=cs: e.tensor_tensor(out=t1, in0=kv_, in1=bc(cs, 1, 2), op=ALU.mult),
                             reads=[pk, "cos"], writes=["scr_c"])
                        S.op("dve", lambda e, kv_=kv_, t2=t2, sn=sn: e.tensor_tensor(out=t2[:, :, 0, :], in0=kv_[:, :, 1, :], in1=bc(sn[:, 0, :], 1, 2), op=ALU.mult),
                             reads=[pk, "sin"], writes=["scr_d"])
                        S.op("dve", lambda e, kv_=kv_, t2=t2, sn=sn: e.tensor_tensor(out=t2[:, :, 1, :], in0=kv_[:, :, 0, :], in1=bc(sn[:, 1, :], 1, 2), op=ALU.mult),
                             reads=[pk, "sin"], writes=["scr_d"])
                        S.op("dve", lambda e: e.tensor_tensor(out=qk_r[:, 512:640], in0=scr[:, 512:640], in1=scr[:, 1152:1280], op=ALU.add),
                             reads=["scr_c", "scr_d"], writes=["qk_r_k"])
                        S.op("act", lambda e, P=P, slotk=slotk: e.activation(out=vb[:, slotk, :, 0:64], in_=P[:, 128:256].rearrange("p (h d) -> p h d", h=2), func=AF.Copy),
                             reads=[pk], writes=["vb%d" % slotk])
                        if main:
                            srcs = [qk_r[:, g * 128:(g + 1) * 128] for g in range(5)]
                            for i, s_ap in enumerate(srcs):
                                S.op("pe", lambda e, s_ap=s_ap, i=i: e.transpose(PT[:, i * 128:(i + 1) * 128], s_ap, idb[:]),
                                     reads=["qk_r", "qk_r_k", "idb"], writes=["PT"])
                            S.op("act", lambda e, c=c: e.activation(out=qT[:, :, c * 128:(c + 1) * 128], in_=PT[:, 0:512].rearrange("p (a b) -> p a b", b=128), func=AF.Copy),
                                 reads=["PT"], writes=[("qT", c)])
                            S.op("act", lambda e, slotk=slotk: e.activation(out=kTb[:, slotk * 128:(slotk + 1) * 128], in_=PT[:, 512:640], func=AF.Copy),
                                 reads=["PT"], writes=["kT%d" % slotk])
                        else:
                            S.op("pe", lambda e: e.transpose(PT[:, 0:128], qk_r[:, 512:640], idb[:]), reads=["qk_r_k", "idb"], writes=["PT"])
                            S.op("act", lambda e: e.activation(out=kTb[:, 0:128], in_=PT[:, 0:128], func=AF.Copy), reads=["PT"], writes=["kT0"])
                    P, pk = nextP()
                    mm_group(P[:, 0:512], pk, [(lhs[k], Wi[:, k, C_MV:C_MV + 512]) for k in range(8)], reads=[("hT", c)] + wk(C_MV))
                    S.op("act", lambda e, P=P, c=c: e.activation(out=mv_aug[:, c, :, 0:128], in_=P[:, 0:512].rearrange("p (h d) -> p h d", h=4), func=AF.Copy),
                         reads=[pk], writes=["mv_aug%d" % c])
                    if main:
                        P, pk = nextP()
                        mm_group(P[:, 0:512], pk, [(lhs[k], Wi[:, k, C_MO:C_MO + 512]) for k in range(8)], reads=[("hT", c)] + wk(C_MO))
                        S.op("act", lambda e, P=P, c=c: e.activation(out=tmo[:, c, :], in_=P[:, 0:512], func=AF.Tanh, scale=0.5), reads=[pk], writes=[("tmo", c)])

                groups = [0, 1, 2, 3] if (main or last_pre) else [2, 3]
                for m in groups:
                    P, pk = nextP()
                    mm_group(P[:, 0:T1], pk, [(Wi[:, k, C_MQ + m * 128:C_MQ + (m + 1) * 128], hT[:, k, :]) for k in range(8)], reads=hTk + wk(C_MQ + m * 128))
                    S.op("act", lambda e, P=P, m=m: e.activation(out=conv_buf[:, m, 3:3 + T1], in_=P[:, 0:T1], func=AF.Copy), reads=[pk, "convbuf"], writes=[("cb", m)])
                    S.op("dve", lambda e, m=m: e.tensor_scalar(out=cacc[:, m, :], in0=conv_buf[:, m, 0:T1], scalar1=convw[:, m, 0:1], scalar2=convb[:, m:m + 1],
                                                                op0=ALU.mult, op1=ALU.add), reads=[("cb", m), "convw", "convb", "convbuf"], writes=[("cacc", m)])
                    for jt in range(1, 4):
                        S.op("dve", lambda e, m=m, jt=jt: e.scalar_tensor_tensor(out=cacc[:, m, :], in0=conv_buf[:, m, jt:jt + T1], scalar=convw[:, m, jt:jt + 1],
                                                                                in1=cacc[:, m, :], op0=ALU.mult, op1=ALU.add),
                             reads=[("cb", m), "convw", ("cacc", m)], writes=[("cacc", m)])
                    S.op("act", lambda e, m=m: e.activation(out=conv_buf[:, m, 0:3], in_=conv_buf[:, m, T1:T1 + 3], func=AF.Copy), reads=[("cb", m)], writes=[("cb", m)])
                    S.op("act", lambda e, m=m: e.activation(out=ctanh[:, m, :], in_=cacc[:, m, :], func=AF.Tanh), reads=[("cacc", m)], writes=[("ctanh", m)])
                    S.op("dve", lambda e, m=m: e.scalar_tensor_tensor(out=mqkT[:, m, :], in0=ctanh[:, m, :], scalar=1.0, in1=cacc[:, m, :], op0=ALU.add, op1=ALU.mult),
                         reads=[("ctanh", m), ("cacc", m)], writes=[("mqkT", m)])
                for c in range(CPT):
                    transpose_to(k_tm[:, c, :], [mqkT[:, 2, c * 128:(c + 1) * 128], mqkT[:, 3, c * 128:(c + 1) * 128]],
                                 reads=[("mqkT", 2), ("mqkT", 3)], writes=[("k_tm", c)], evac="dve")

                NG = CPT * 4
                S.op("dve", lambda e: e.tensor_tensor(out=gb[:], in0=gates_tm[:], in1=bc(bif[:, 0:8], 1, CPT), op=ALU.add), reads=["gates_tm", "bif"], writes=["gb"])
                S.op("act", lambda e: e.activation(out=lfp[:], in_=gb[:, :, 4:8], func=AF.Exp, scale=-1.0), reads=["gb"], writes=["lfp"])
                S.op("act", lambda e: e.activation(out=lfp[:], in_=lfp[:], func=AF.Ln, bias=1.0), reads=["lfp"], writes=["lfp"])
                P, pk = nextP()
                S.op("pe", lambda e, P=P: e.matmul(P[:, 0:NG], lhsT=trif[:], rhs=lfp[:].rearrange("p c h -> p (c h)"), start=True, stop=True),
                     reads=["lfp", "trif"], writes=[pk])
                abv = ab[:].rearrange("p k (c h) -> p k c h", c=CPT)
                S.op("dve", lambda e, P=P: e.tensor_tensor(out=abv[:, 0, :, :], in0=P[:, 0:NG].rearrange("p (c h) -> p c h", c=CPT), in1=gb[:, :, 0:4], op=ALU.add),
                     reads=[pk, "gb"], writes=["ab"])
                S.op("dve", lambda e, P=P: e.tensor_copy(out=ab[:, 1, :], in_=P[:, 0:NG]), reads=[pk], writes=["ab"])
                P2, pk2 = nextP()
                for c in range(CPT):
                    S.op("pe", lambda e, P2=P2, c=c: e.matmul(P2[0:4, c * 128:(c + 1) * 128], lhsT=ab[:, 0, c * 4:(c + 1) * 4], rhs=idf[:], start=True, stop=True),
                         reads=["ab", "idf"], writes=[pk2])
                for c in range(CPT):
                    S.op("pe", lambda e, P2=P2, c=c: e.matmul(P2[0:4, 256 + c * 128:256 + (c + 1) * 128], lhsT=ab[:, 1, c * 4:(c + 1) * 4], rhs=idf[:], start=True, stop=True),
                         reads=["ab", "idf"], writes=[pk2])
                S.op("dve", lambda e, P2=P2: e.reduce_max(out=hp[:, 0:CPT], in_=P2[0:4, 0:CPT * 128].rearrange("p (c t) -> p c t", c=CPT), axis=AX.X),
                     reads=[pk2], writes=["hp"])
                S.op("dve", lambda e, P2=P2: e.tensor_copy(out=hp[:, 4:4 + CPT], in_=P2[0:4, 256:256 + CPT * 128].rearrange("p (c t) -> p c t", c=CPT)[:, :, 127]), reads=[pk2], writes=["hp"])
                for c in range(CPT):
                    S.op("dve", lambda e, c=c: e.tensor_tensor(out=hp[:, 8 + c:9 + c], in0=mst[:], in1=hp[:, c:c + 1], op=ALU.max), reads=["hp", "mst"], writes=["hp"])
                    S.op("dve", lambda e, c=c: e.tensor_tensor(out=hp[:, 12 + c:13 + c], in0=mst[:], in1=hp[:, 8 + c:9 + c], op=ALU.subtract), reads=["hp", "mst"], writes=["hp"])
                    S.op("dve", lambda e, c=c: e.tensor_tensor(out=mst[:], in0=hp[:, 8 + c:9 + c], in1=hp[:, 4 + c:5 + c], op=ALU.subtract), reads=["hp"], writes=["mst"])
                S.op("act", lambda e: e.activation(out=hp[:, 12:12 + CPT], in_=hp[:, 12:12 + CPT], func=AF.Exp), reads=["hp"], writes=["hp"])
                hp_p = hp[:].ap[0][0]
                muf = AP(hp[:].tensor, hp[:].offset + 8, [[hp_p, 4], [4, 2], [1, CPT], [0, 4]])
                idf_p = idf[:].ap[0][0]
                i4 = AP(idf[:].tensor, idf[:].offset, [[idf_p, 4], [0, 2], [0, CPT], [1, 4]])
                S.op("dve", lambda e: e.tensor_tensor(out=Rm[0:4, :, :].rearrange("p (k c) h -> p k c h", k=2), in0=muf, in1=i4, op=ALU.mult), reads=["hp", "idf"], writes=["Rm"])
                P3, pk3 = nextP()
                S.op("pe", lambda e, P3=P3: e.matmul(P3[:, 0:2 * NG], lhsT=onesf[:, :], rhs=Rm[:].rearrange("p a h -> p (a h)"), start=True, stop=True),
                     reads=["Rm", "onesf"], writes=[pk3])
                S.op("dve", lambda e, P3=P3: e.tensor_tensor(out=d2[:], in0=ab[:], in1=bc(P3[:, 0:NG], 1, 2), op=ALU.subtract), reads=[pk3, "ab"], writes=["d2"])
                S.op("dve", lambda e, P3=P3: e.tensor_copy(out=fbc[:], in_=P3[:, NG:2 * NG]), reads=[pk3], writes=["fbc"])
                S.op("act", lambda e: e.activation(out=ew[:], in_=d2[:], func=AF.Exp), reads=["d2"], writes=["ew"])
                S.op("dve", lambda e: e.tensor_scalar(out=wq[:], in0=ew[:, 0, :], scalar1=0.125, scalar2=None, op0=ALU.mult), reads=["ew"], writes=["wq"])

                for c in range(CPT):
                    jg = ti * CPT + c
                    if main and _DBG.get("attn", 1):
                        for kvg in range(2):
                            PYt, pyk = PY[kvg], ("PY", kvg)
                            rows = slice(kvg * 64, (kvg + 1) * 64)
                            for g in range(4):
                                S.op("pe", lambda e, g=g, PYt=PYt, rows=rows, c=c: e.matmul(PYt[:, g * 256:(g + 1) * 256], lhsT=qT[rows, g, c * 128:(c + 1) * 128],
                                                                                          rhs=kTb[rows, c * 128:(c + 2) * 128], start=True, stop=True),
                                     reads=[("qT", c), "kT%d" % c, "kT%d" % (c + 1)], writes=[pyk])
                            sc3 = PYt[:, :].rearrange("p (g k) -> p g k", g=4)
                            o = 16 + kvg * 16
                            S.op("dve", lambda e, sc3=sc3, o=o: e.reduce_max(out=st[:, o:o + 4], in_=sc3, axis=AX.X), reads=[pyk], writes=["st_a%d" % kvg])
                            S.op("dve", lambda e, o=o: e.tensor_scalar(out=st[:, o + 4:o + 8], in0=st[:, o:o + 4], scalar1=-0.125, scalar2=None, op0=ALU.mult),
                                 reads=["st_a%d" % kvg], writes=["st_a%d" % kvg])
                            for g in range(4):
                                S.op("act", lambda e, g=g, PYt=PYt, o=o: e.activation(out=pexp[:, g, :], in_=PYt[:, g * 256:(g + 1) * 256], func=AF.Exp,
                                                                                      scale=0.125, bias=st[:, o + 4 + g:o + 5 + g]),
                                     reads=[pyk, "st_a%d" % kvg], writes=["pexp"])
                            mi = 1 if jg == 0 else 0
                            S.op("dve", lambda e, mi=mi: e.tensor_tensor(out=pexp[:], in0=pexp[:], in1=bc(mask[:, mi, :], 1, 4), op=ALU.mult),
                                 reads=["pexp", "mask"], writes=["pexp"])
                            for kb in range(2):
                                for g in range(4):
                                    i = kb * 4 + g
                                    S.op("pe", lambda e, g=g, kb=kb, i=i: e.transpose(PT[:, i * 128:(i + 1) * 128], pexp[:, g, kb * 128:(kb + 1) * 128], idb[:]),
                                         reads=["pexp", "idb"], writes=["PT"])
                            S.op("act", lambda e: e.activation(out=pTt[:].rearrange("p a g q -> p (a g q)"), in_=PT[:, 0:1024], func=AF.Copy), reads=["PT"], writes=["pTt"])
                            P, pk = nextP()
                            for g in range(4):
                                for kb in range(2):
                                    S.op("pe", lambda e, g=g, kb=kb, P=P, kvg=kvg, c=c: e.matmul(P[:, g * 65:(g + 1) * 65], lhsT=pTt[:, kb, g, :], rhs=vb[:, c + kb, kvg, 0:65],
                                                                                              start=(kb == 0), stop=(kb == 1)),
                                         reads=["pTt", "vb%d" % (c + kb)], writes=[pk])
                            S.op("dve", lambda e, o=o, kvg=kvg: e.tensor_tensor(out=st[:, o + 8:o + 12], in0=st[:, o + 4:o + 8], in1=sinks[:, kvg * 4:(kvg + 1) * 4], op=ALU.add),
                                 reads=["st_a%d" % kvg, "sinks"], writes=["st_a%d" % kvg])
                            S.op("act", lambda e, o=o: e.activation(out=st[:, o + 8:o + 12], in_=st[:, o + 8:o + 12], func=AF.Exp), reads=["st_a%d" % kvg], writes=["st_a%d" % kvg])
                            o3 = P[:, 0:260].rearrange("p (g d) -> p g d", g=4)
                            S.op("dve", lambda e, o=o, o3=o3: e.tensor_tensor(out=st[:, o + 8:o + 12], in0=o3[:, :, 64], in1=st[:, o + 8:o + 12], op=ALU.add),
                                 reads=[pk, "st_a%d" % kvg], writes=["st_a%d" % kvg])
                            S.op("dve", lambda e, o=o: e.reciprocal(out=st[:, o + 12:o + 16], in_=st[:, o + 8:o + 12]), reads=["st_a%d" % kvg], writes=["st_a%d" % kvg])
                            S.op("dve", lambda e, o=o, o3=o3, kvg=kvg: e.tensor_tensor(out=attn_o[:, kvg * 256:(kvg + 1) * 256].rearrange("p (g d) -> p g d", g=4),
                                                                                     in0=o3[:, :, 0:64], in1=bc(st[:, o + 12:o + 16], 2, 64), op=ALU.mult),
                                 reads=[pk, "st_a%d" % kvg], writes=["attn_o"])
                        transpose_to(attn_oT[:, :, c * 128:(c + 1) * 128], [attn_o[:, k * 128:(k + 1) * 128] for k in range(4)],
                                     reads=["attn_o"], writes=[("attn_oT", c)])

                    kt3 = k_tm[:, c, :].rearrange("p (h d) -> p h d", h=4)
                    S.op("dve", lambda e, kt3=kt3, c=c: e.tensor_tensor(out=kw[:], in0=kt3, in1=bc(wq[:, c * 4:(c + 1) * 4], 2, 64), op=ALU.mult),
                         reads=[("k_tm", c), "wq"], writes=["kw"])
                    fb_p = fbc[:].ap[0][0]
                    for par in range(2):
                        rows = slice(par * 64, (par + 1) * 64)
                        fsel = AP(fbc[:].tensor, fbc[:].offset + par * 64 * fb_p + c * 4 + par, [[fb_p, 64], [2, 2], [0, 130]])
                        S.op("dve", lambda e, rows=rows, fsel=fsel: e.tensor_tensor(out=Chat[rows, :, :], in0=Cst[rows, :, :], in1=fsel, op=ALU.mult),
                             reads=["Cst", "fbc"], writes=["Chat"])
                    if main:
                        cz_p = Chat_z[:].ap[0][0]
                        for par in range(2):
                            rows = slice(par * 64, (par + 1) * 64)
                            czsel = AP(Chat_z[:].tensor, Chat_z[:].offset + par * 64 * cz_p + par * 130, [[cz_p, 64], [260, 2], [1, 130]])
                            S.op("act", lambda e, rows=rows, czsel=czsel: e.activation(out=czsel, in_=Chat[rows, :, :], func=AF.Copy), reads=["Chat"], writes=["Chat_bf"])
                    PD, pdk = PY[1], ("PY", 1)
                    for h in range(4):
                        pr = h // 2
                        S.op("pe", lambda e, h=h, pr=pr, c=c, PD=PD: e.matmul(PD[:, h * 256:h * 256 + 129], lhsT=kw[:, 2 * pr:2 * pr + 2, :].rearrange("p a d -> p (a d)"),
                                                                           rhs=mv_aug[:, c, h, 0:129], start=True, stop=True),
                             reads=["kw", "mv_aug%d" % c], writes=[pdk])
                    mlvl = _DBG.get("mlm", 9)
                    mm_main = main and mlvl
                    if mm_main:
                        Pp = [nextP(), nextP()]
                        for h in range(4):
                            rows = slice((h % 2) * 64, (h % 2) * 64 + 64)
                            P, pk = Pp[h % 2]
                            S.op("pe", lambda e, h=h, rows=rows, P=P, c=c: e.matmul(P[:, (h // 2) * 128:(h // 2 + 1) * 128], lhsT=mqkT[rows, 2 + h // 2, c * 128:(c + 1) * 128],
                                                                                  rhs=mqkT[rows, h // 2, c * 128:(c + 1) * 128], start=True, stop=True),
                                 reads=[("mqkT", 0), ("mqkT", 1), ("mqkT", 2), ("mqkT", 3)], writes=[pk])
                        for h in range(4):
                            P, pk = Pp[h % 2]
                            S.op("dve", lambda e, h=h, P=P, c=c: e.scalar_tensor_tensor(out=STw[:, h, :], in0=P[:, (h // 2) * 128:(h // 2 + 1) * 128], scalar=wq[:, c * 4 + h:c * 4 + h + 1],
                                                                                       in1=trib[:], op0=ALU.mult, op1=ALU.mult),
                                 reads=[pk, "wq", "trib"], writes=["STw"])
                        PN, pnk = PY[0], ("PY", 0)
                        for h in (range(4) if mlvl >= 2 else []):
                            rows = slice((h % 2) * 64, (h % 2) * 64 + 64)
                            S.op("pe", lambda e, h=h, PN=PN, c=c: e.matmul(PN[:, h * 256:h * 256 + 129], lhsT=STw[:, h, :], rhs=mv_aug[:, c, h, 0:129], start=True, stop=False),
                                 reads=["STw", "mv_aug%d" % c], writes=[pnk])
                            S.op("pe", lambda e, h=h, PN=PN, c=c: e.matmul(PN[:, h * 256:h * 256 + 129], lhsT=mqkT[:, h // 2, c * 128:(c + 1) * 128],
                                                                         rhs=Chat_z[:, h, 0:129], start=False, stop=True),
                                 reads=[("mqkT", 0), ("mqkT", 1), "Chat_bf"], writes=[pnk])
                    PD3 = PD[:, :].rearrange("p (h x) -> p h x", h=4)
                    for par in range(2):
                        rows = slice(par * 64, (par + 1) * 64)
                        pd_p = PD[:, :].ap[0][0]
                        dsel = AP(PD[:, :].tensor, PD[:, :].offset + par * 64 * pd_p + par * 256, [[pd_p, 64], [512, 2], [1, 129]])
                        S.op("dve", lambda e, rows=rows, dsel=dsel: e.tensor_tensor(out=Cst[rows, :, 0:129], in0=Chat[rows, :, 0:129], in1=dsel, op=ALU.add),
                             reads=[pdk, "Chat"], writes=["Cst"])
                    if mm_main and mlvl >= 3:
                        PN3 = PN[:, :].rearrange("p (h x) -> p h x", h=4)
                        o = 48
                        S.op("dve", lambda e, PN3=PN3: e.tensor_copy(out=st[:, o + 12:o + 16], in_=PN3[:, :, 128]), reads=[pnk], writes=["st_m"])
                        S.op("dve", lambda e: e.scalar_tensor_tensor(out=st[:, o:o + 4], in0=st[:, o + 12:o + 16], scalar=-1.0, in1=st[:, o + 12:o + 16], op0=ALU.mult, op1=ALU.max),
                             reads=["st_m"], writes=["st_m"])
                        S.op("dve", lambda e, c=c: e.tensor_tensor(out=st[:, o:o + 4], in0=st[:, o:o + 4], in1=ew[:, 1, c * 4:(c + 1) * 4], op=ALU.max),
                             reads=["st_m", "ew"], writes=["st_m"])
                        S.op("dve", lambda e: e.reciprocal(out=st[:, o:o + 4], in_=st[:, o:o + 4]), reads=["st_m"], writes=["st_m"])
                        S.op("dve", lambda e: e.memset(st[:, o + 4:o + 8], 0.0), reads=["st_m"], writes=["st_m"])
                        for h in range(4):
                            S.op("act", lambda e, h=h, PN3=PN3: e.activation(out=junk[:, 0:128], in_=PN3[:, h, 0:128], func=AF.Square,
                                                                            accum_out=st[:, o + 4 + h:o + 5 + h]),
                                 reads=[pnk, "st_m"], writes=["junk", "st_m"])
                        S.op("dve", lambda e: e.tensor_tensor(out=st[:, o + 4:o + 8], in0=st[:, o + 4:o + 8], in1=st[:, o:o + 4], op=ALU.mult), reads=["st_m"], writes=["st_m"])
                        S.op("dve", lambda e: e.tensor_tensor(out=st[:, o + 4:o + 8], in0=st[:, o + 4:o + 8], in1=st[:, o:o + 4], op=ALU.mult), reads=["st_m"], writes=["st_m"])
                        rstd_from_ssq(o + 4, 4, 1.0 / 128, "st_m")
                        S.op("dve", lambda e: e.tensor_tensor(out=st[:, o + 8:o + 12], in0=st[:, o:o + 4], in1=st[:, o + 4:o + 8], op=ALU.mult), reads=["st_m"], writes=["st_m"])
                        for h in range(4):
                            S.op("dve", lambda e, h=h, PN3=PN3: e.scalar_tensor_tensor(out=cellg[:, h, :], in0=PN3[:, h, 0:128], scalar=st[:, o + 8 + h:o + 9 + h],
                                                                                      in1=hng[:, h * 128:(h + 1) * 128], op0=ALU.mult, op1=ALU.mult),
                                 reads=[pnk, "st_m", "hng"], writes=["cellg"])
                        S.op("dve", lambda e, c=c: e.scalar_tensor_tensor(out=mo_out[:], in0=tmo[:, c, :], scalar=1.0, in1=cellg[:].rearrange("p h d -> p (h d)"),
                                                                         op0=ALU.add, op1=ALU.mult),
                             reads=[("tmo", c), "cellg"], writes=["mo_out"])
                        if mlvl >= 4:
                            transpose_to(mlstm_oT[:, :, c * 128:(c + 1) * 128], [mo_out[:, k * 128:(k + 1) * 128] for k in range(4)],
                                         reads=["mo_out"], writes=[("mlstm_oT", c)])

                if not main or not _DBG.get("merge", 1):
                    return
                S.op("act", lambda e: e.activation(out=kTb[:, 0:128], in_=kTb[:, CPT * 128:(CPT + 1) * 128], func=AF.Copy), reads=["kT%d" % CPT], writes=["kT0"])
                S.op("act", lambda e: e.activation(out=vb[:, 0, :, :], in_=vb[:, CPT, :, :], func=AF.Copy), reads=["vb%d" % CPT], writes=["vb0"])

                aoT = [("attn_oT", c) for c in range(CPT)]
                moT = [("mlstm_oT", c) for c in range(CPT)]
                for ob in range(8):
                    X, xk = nextP()
                    mm_group(X[:, 0:T1], xk, [(Wi[:, k, C_GA + ob * 128:C_GA + (ob + 1) * 128], hT[:, k, :]) for k in range(8)], reads=hTk + wk(C_GA + ob * 128))
                    mm_group(X[:, T1:2 * T1], xk, [(Wi[:, k, C_GM + ob * 128:C_GM + (ob + 1) * 128], hT[:, k, :]) for k in range(8)], reads=hTk + wk(C_GM + ob * 128))
                    S.op("act", lambda e, X=X: e.activation(out=tg[:, 0:2 * T1], in_=X[:, 0:2 * T1], func=AF.Tanh, scale=0.5), reads=[xk], writes=["tg"])
                    Y, yk = nextP()
                    mm_group(Y[:, 0:T1], yk, [(Wab[:, k, ob * 128:(ob + 1) * 128], attn_oT[:, k, :]) for k in range(4)], reads=aoT + ["Wab"])
                    mm_group(Y[:, T1:2 * T1], yk, [(Wmb[:, k, ob * 128:(ob + 1) * 128], mlstm_oT[:, k, :]) for k in range(4)], reads=moT + ["Wmb"])
                    S.op("dve", lambda e, Y=Y: e.scalar_tensor_tensor(out=scr[:, 0:2 * T1], in0=tg[:, 0:2 * T1], scalar=1.0, in1=Y[:, 0:2 * T1], op0=ALU.add, op1=ALU.mult),
                         reads=["tg", yk], writes=["scr_a"])
                    S.op("dve", lambda e, ob=ob: e.tensor_tensor(out=mergedT[:, ob, :], in0=scr[:, 0:T1], in1=scr[:, T1:2 * T1], op=ALU.add),
                         reads=["scr_a"], writes=[("mergedT", ob)])
                mk_ = [("mergedT", ob) for ob in range(8)]
                S.op("dve", lambda e: e.memset(st[:, 8:16], 0.0), writes=["st_o"])
                for c in range(CPT):
                    PYt, pyk = PY[c % 2], ("PY", c % 2)
                    for nh in range(2):
                        mm_group(PYt[:, nh * 512:(nh + 1) * 512], pyk, [(mergedT[:, k, c * 128:(c + 1) * 128], Wo[:, k, nh * 512:(nh + 1) * 512]) for k in range(8)],
                                 reads=mk_ + ["Wo"])
                    for nh in range(2):
                        S.op("act", lambda e, PYt=PYt, c=c, nh=nh: e.activation(out=junk[:, 0:512], in_=PYt[:, nh * 512:(nh + 1) * 512], func=AF.Square,
                                                                               accum_out=st[:, 8 + 2 * c + nh:9 + 2 * c + nh]),
                             reads=[pyk, "st_o"], writes=["junk", "st_o"])
                    S.op("dve", lambda e, c=c: e.tensor_tensor(out=st[:, 12 + c:13 + c], in0=st[:, 8 + 2 * c:9 + 2 * c], in1=st[:, 9 + 2 * c:10 + 2 * c], op=ALU.add),
                         reads=["st_o"], writes=["st_o"])
                    S.op("dve", lambda e, c=c: e.tensor_scalar(out=st[:, 12 + c:13 + c], in0=st[:, 12 + c:13 + c], scalar1=1.0 / D, scalar2=EPS, op0=ALU.mult, op1=ALU.add),
                         reads=["st_o"], writes=["st_o"])
                    S.op("act", lambda e, c=c: e.activation(out=st[:, 12 + c:13 + c], in_=st[:, 12 + c:13 + c], func=AF.Ln), reads=["st_o"], writes=["st_o"])
                    S.op("act", lambda e, c=c: e.activation(out=st[:, 12 + c:13 + c], in_=st[:, 12 + c:13 + c], func=AF.Exp, scale=-0.5), reads=["st_o"], writes=["st_o"])
                    S.op("dve", lambda e, PYt=PYt, c=c: e.scalar_tensor_tensor(out=scr[:, 0:D], in0=PYt[:, :], scalar=st[:, 12 + c:13 + c], in1=gains[:, 1, :],
                                                                              op0=ALU.mult, op1=ALU.mult),
                         reads=[pyk, "st_o", "gains"], writes=["scr_a", "scr_b", "scr_c"])
                    S.op("dve", lambda e, c=c: e.tensor_tensor(out=xs[:, c, :], in0=xs[:, c, :], in1=scr[:, 0:D], op=ALU.add),
                         reads=[("x", slot, c), "scr_a", "scr_b", "scr_c"], writes=[("x", slot, c)])
                    r0 = (ti * CPT + c) * 128
                    S.dma("sp", out[r0:r0 + 128, :], xs[:, c, :], reads=[("x", slot, c)], writes=[("out", r0)])

            load_x(xp, 0)
            for ti in range(_DBG["n_pre"]):
                if ti + 1 < _DBG["n_pre"]:
                    load_x(xp, ti + 1)
                else:
                    load_x(xm, 0)
                flush(len(pending) if ti + 2 >= _DBG["n_pre"] else 8)
                mix_tile(ti, True, ti == _DBG["n_pre"] - 1)
            S.op("dve", lambda e: e.tensor_scalar(out=Cst[:], in0=Cst[:], scalar1=flag[:, 0:1], scalar2=None, op0=ALU.mult), reads=["Cst", "flag"], writes=["Cst"])
            S.op("dve", lambda e: e.tensor_scalar(out=mst[:], in0=mst[:], scalar1=flag[0:4, 0:1], scalar2=None, op0=ALU.mult), reads=["mst", "flag"], writes=["mst"])
            for ti in range(_DBG["n_main"]):
                if ti + 1 < _DBG["n_main"]:
                    load_x(xm, ti + 1)
                mix_tile(ti, False, False)
            S.barrier()
            S.emit()

        with ExitStack() as e2:
            def sb2(name, shape, dt):
                return e2.enter_context(nc.sbuf_tensor(name, list(shape), dt))

            Wfi = sb2("Wfi", [128, 8, 2 * DFF], BF16)
            Wfo = sb2("Wfo", [128, NJB, D], BF16)
            gains2 = sb2("gains2", [128, 2, D], F32)
            idb2 = sb2("idb2", [128, 128], BF16)
            x1 = sb2("x1", [128, T2 // 128, D], F32)
            h2 = sb2("h2", [128, D], BF16)
            junk2 = sb2("junk2", [128, D], BF16)
            h2T = sb2("h2T", [128, 8, T2], BF16)
            actT = sb2("actT", [128, NJB, T2], BF16)
            sg = [sb2("sg%d" % i, [128, T2], BF16) for i in range(2)]
            r2 = sb2("r2", [128, D], F32)
            st2 = sb2("st2", [128, 64], F32)

            S.dma("sp", idb2[:], idb_d, writes=["idb2"])
            S.dma("sp", gains2[:], gains_d[:, 2:4, :], writes=["gains2"])
            w_fi_v = w_fi.rearrange("(kc p) n -> p kc n", p=128)
            w_fo_v = w_fo.rearrange("(jb p) n -> p jb n", p=128)
            stg2 = sb2("stg2", [128, 2, 1024], F32)
            stg2_i = [0]
            pending2 = []

            def load_cast2(dst, src, wkeys_):
                def go():
                    sl = stg2_i[0] % 2
                    stg2_i[0] += 1
                    shp = list(dst.shape)
                    v = stg2[:, sl, :]
                    if len(shp) == 3:
                        v = v.rearrange("p (a b) -> p a b", a=shp[1])
                    eng = "act" if (stg2_i[0] % 2) else "dve"
                    S.dma("sp", v, src, writes=[("stg2", sl)])
                    if eng == "act":
                        S.op("act", lambda e: e.activation(out=dst, in_=v, func=AF.Copy), reads=[("stg2", sl)], writes=wkeys_)
                    else:
                        S.op("dve", lambda e: e.tensor_copy(out=dst, in_=v), reads=[("stg2", sl)], writes=wkeys_)
                pending2.append(go)

            for jb in range(NJB):
                for half in range(2):
                    c0 = half * DFF + jb * 128
                    load_cast2(Wfi[:, :, c0:c0 + 128], w_fi_v[:, :, c0:c0 + 128], [("Wfi", half, jb)])
            for jb in range(NJB):
                load_cast2(Wfo[:, jb, :], w_fo_v[:, jb, :], [("Wfo", jb)])

            NS = T2 // 128
            h2T_alt = stg2[:].rearrange("p a b -> p (a b)").bitcast(BF16).rearrange("p (k t) -> p k t", k=8)
            h2Tb = [h2T[:], h2T_alt]
            ALIAS = [("stg2", 0), ("stg2", 1)]

            def head(ti, c):
                b = ti % 2
                r0 = (ti * NS + c) * 128
                S.dma("sp", x1[:, c, :], out[r0:r0 + 128, :], reads=[("out", r0)], writes=[("x1", c)])
                if ti == 0 and c == NS - 1:
                    while pending2:
                        pending2.pop(0)()
                col = 32 + c
                k2 = ("st2h", c)
                S.op("dve", lambda e: e.memset(st2[:, col:col + 1], 0.0), writes=[k2])
                S.op("act", lambda e: e.activation(out=junk2[:], in_=x1[:, c, :], func=AF.Square, accum_out=st2[:, col:col + 1]),
                     reads=[("x1", c), k2], writes=["junk2", k2])
                S.op("dve", lambda e: e.tensor_scalar(out=st2[:, col:col + 1], in0=st2[:, col:col + 1], scalar1=1.0 / D, scalar2=EPS, op0=ALU.mult, op1=ALU.add), reads=[k2], writes=[k2])
                S.op("act", lambda e: e.activation(out=st2[:, col:col + 1], in_=st2[:, col:col + 1], func=AF.Ln), reads=[k2], writes=[k2])
                S.op("act", lambda e: e.activation(out=st2[:, col:col + 1], in_=st2[:, col:col + 1], func=AF.Exp, scale=-0.5), reads=[k2], writes=[k2])
                S.op("dve", lambda e: e.scalar_tensor_tensor(out=h2[:], in0=x1[:, c, :], scalar=st2[:, col:col + 1], in1=gains2[:, 0, :], op0=ALU.mult, op1=ALU.mult),
                     reads=[("x1", c), k2, "gains2"], writes=["h2"])
                for k in range(8):
                    S.op("pe", lambda e, k=k: e.transpose(PT[:, k * 128:(k + 1) * 128], h2[:, k * 128:(k + 1) * 128], idb2[:]), reads=["h2", "idb2"], writes=["PT"])
                dstT = h2Tb[b][:, :, c * 128:(c + 1) * 128]
                S.op("act", lambda e: e.activation(out=dstT, in_=PT[:, 0:1024].rearrange("p (a b) -> p a b", b=128), func=AF.Copy),
                     reads=["PT"], writes=[("h2T", b, c)] + (ALIAS if b == 1 else []))

            def tail(ti, c):
                PYt, pyk = PY[c % 2], ("PY", c % 2)
                ak_ = [("actT", jb) for jb in range(NJB)]
                kb = ("st2b", c)
                S.op("dve", lambda e: e.memset(st2[:, 8 + 4 * c:12 + 4 * c], 0.0), writes=[kb])
                for nh in range(2):
                    mm_group(PYt[:, nh * 512:(nh + 1) * 512], pyk, [(actT[:, jb, c * 128:(c + 1) * 128], Wfo[:, jb, nh * 512:(nh + 1) * 512]) for jb in range(NJB)],
                             reads=ak_ + [("Wfo", jb) for jb in range(NJB)])
                o = 8 + 4 * c
                for nh in range(2):
                    S.op("act", lambda e, nh=nh: e.activation(out=junk2[:, 0:512], in_=PYt[:, nh * 512:(nh + 1) * 512], func=AF.Square, accum_out=st2[:, o + nh:o + nh + 1]),
                         reads=[pyk, kb], writes=["junk2", kb])
                S.op("dve", lambda e: e.tensor_tensor(out=st2[:, o + 2:o + 3], in0=st2[:, o:o + 1], in1=st2[:, o + 1:o + 2], op=ALU.add), reads=[kb], writes=[kb])
                S.op("dve", lambda e: e.tensor_scalar(out=st2[:, o + 2:o + 3], in0=st2[:, o + 2:o + 3], scalar1=1.0 / D, scalar2=EPS, op0=ALU.mult, op1=ALU.add), reads=[kb], writes=[kb])
                S.op("act", lambda e: e.activation(out=st2[:, o + 2:o + 3], in_=st2[:, o + 2:o + 3], func=AF.Ln), reads=[kb], writes=[kb])
                S.op("act", lambda e: e.activation(out=st2[:, o + 2:o + 3], in_=st2[:, o + 2:o + 3], func=AF.Exp, scale=-0.5), reads=[kb], writes=[kb])
                S.op("dve", lambda e: e.scalar_tensor_tensor(out=r2[:], in0=PYt[:, :], scalar=st2[:, o + 2:o + 3], in1=gains2[:, 1, :], op0=ALU.mult, op1=ALU.mult),
                     reads=[pyk, kb, "gains2"], writes=["r2"])
                S.op("dve", lambda e: e.tensor_tensor(out=x1[:, c, :], in0=x1[:, c, :], in1=r2[:], op=ALU.add), reads=[("x1", c), "r2"], writes=[("x1", c)])
                r0 = (ti * NS + c) * 128
                S.dma("sp", out[r0:r0 + 128, :], x1[:, c, :], reads=[("x1", c)], writes=[("out", r0)])

            np2 = _DBG["p2"]
            for c in range(NS):
                if np2 > 0:
                    head(0, c)
            for ti in range(np2):
                b = ti % 2
                h2T_c = h2Tb[b]
                h2k = [("h2T", b, c) for c in range(NS)] + (ALIAS if b == 1 else [])
                for jb in range(NJB):
                    G, gk = nextP()
                    mm_group(G[:, 0:T2], gk, [(Wfi[:, k, jb * 128:(jb + 1) * 128], h2T_c[:, k, :]) for k in range(8)], reads=h2k + [("Wfi", 0, jb)])
                    U, uk = PY[jb % 2][:, 0:512], ("PY", jb % 2)
                    mm_group(U[:, 0:T2], uk, [(Wfi[:, k, DFF + jb * 128:DFF + (jb + 1) * 128], h2T_c[:, k, :]) for k in range(8)], reads=h2k + [("Wfi", 1, jb)])
                    sgt = sg[jb % 2]
                    S.op("act", lambda e, G=G, sgt=sgt: e.activation(out=sgt[:], in_=G[:, 0:T2], func=AF.Silu), reads=[gk], writes=[("sg", jb % 2)])
                    S.op("dve", lambda e, U=U, sgt=sgt, jb=jb: e.tensor_tensor(out=actT[:, jb, :], in0=sgt[:], in1=U[:, 0:T2], op=ALU.mult),
                         reads=[("sg", jb % 2), uk], writes=[("actT", jb)])
                for c in range(NS):
                    tail(ti, c)
                    if ti + 1 < np2 and c >= 1:
                        head(ti + 1, c - 1)
                if ti + 1 < np2:
                    head(ti + 1, NS - 1)
            S.barrier()
            S.emit()
    return nc


_CACHE = {}
_DBG = {"n_pre": NT1, "n_main": NT1, "p2": NT2}


def _consts(half):
    f32 = np.float32
    inv_freq = (10000.0 ** (-np.arange(0, 64, 2, dtype=f32) / f32(64))).astype(f32)
    pos = (half * HALF - 128 + np.arange((NCH + 1) * 128)).astype(f32)
    ang = (pos[:, None] * inv_freq[None, :]).astype(f32)
    emb = np.concatenate([ang, ang], axis=-1)
    cos = np.cos(emb).astype(f32)
    sin = np.sin(emb).astype(f32)
    sgn = np.concatenate([-np.ones(32, f32), np.ones(32, f32)])
    sinS = sin * sgn[None, :]
    cosT = np.ascontiguousarray(cos[:, :32].reshape(NCH + 1, 128, 32).transpose(1, 0, 2))
    sinT = np.ascontiguousarray(sinS.reshape(NCH + 1, 128, 64).transpose(1, 0, 2))
    q = np.arange(128)[:, None]
    kj = np.arange(256)[None, :]
    rel = q + 128 - kj
    band = ((rel >= 0) & (rel < 128)).astype(f32)
    first = band.copy()
    if half == 0:
        first[:, :128] = 0.0
    mask = np.stack([band, first], axis=1).astype(ml_dtypes.bfloat16)
    tri = (np.arange(128)[:, None] <= np.arange(128)[None, :]).astype(f32)
    return dict(cosT=cosT, sinS=sinT, mask=mask, idb=np.eye(128).astype(ml_dtypes.bfloat16), idf=np.eye(128, dtype=f32),
                trif=tri, trib=tri.astype(ml_dtypes.bfloat16), onesf=np.ones((128, 128), f32),
                flag=np.full((128, 1), float(half), f32))


def kernel(x, norm_pre_mix, norm_post_mix, norm_pre_ffn, norm_post_ffn, w_in, attn_sinks, conv_w, conv_b,
           b_igate, b_fgate, mlstm_head_norm, w_attn_branch, w_mlstm_branch, w_out, w_ffn_in, w_ffn_out):
    f32 = np.float32
    x = np.asarray(x, f32)
    rb = lambda v, n: np.ascontiguousarray(np.broadcast_to(np.asarray(v, f32).reshape(1, n), (128, n)))
    gains = np.ascontiguousarray(np.stack([rb(norm_pre_mix[0], D), rb(norm_post_mix[0], D), rb(norm_pre_ffn[0], D), rb(norm_post_ffn[0], D)], axis=1))
    convw = np.ascontiguousarray(np.asarray(conv_w[0], f32).reshape(4, 4, 128).transpose(2, 1, 0))
    convb = np.ascontiguousarray(np.asarray(conv_b[0], f32).reshape(4, 128).T)
    shared = dict(
        w_in=np.ascontiguousarray(w_in[0], f32), w_ab=np.ascontiguousarray(w_attn_branch[0], f32), w_mb=np.ascontiguousarray(w_mlstm_branch[0], f32),
        w_o=np.ascontiguousarray(w_out[0], f32), w_fi=np.ascontiguousarray(w_ffn_in[0], f32), w_fo=np.ascontiguousarray(w_ffn_out[0], f32),
        gains=gains, hng=rb(mlstm_head_norm[0], 512), sinks=rb(attn_sinks[0], 8),
        bif=rb(np.concatenate([np.asarray(b_igate[0], f32), np.asarray(b_fgate[0], f32)]), 8), convw=convw, convb=convb)
    zeros = np.zeros((HALF, D), f32)
    in_maps = []
    for c in range(NCORES):
        b, half = c // 2, c % 2
        m = dict(shared)
        m["xm"] = np.ascontiguousarray(x[b, half * HALF:(half + 1) * HALF])
        m["xp"] = np.ascontiguousarray(x[b, 0:HALF]) if half == 1 else zeros
        m.update(_consts(half))
        in_maps.append(m)
    if _CACHE.get("maps_only"):
        return in_maps
    if "nc" not in _CACHE:
        _CACHE["nc"] = build_program()
    res = run_bass_kernel_spmd(_CACHE["nc"], in_maps, core_ids=list(range(NCORES)))
    outp = np.empty((4, 2 * HALF, D), f32)
    for c in range(NCORES):
        b, half = c // 2, c % 2
        outp[b, half * HALF:(half + 1) * HALF] = res.results[c]["out"]
    return outp
```

```python
import numpy as np
import ml_dtypes
from contextlib import ExitStack
import concourse.bass as bass
import concourse.mybir as mybir
from concourse.bass_utils import run_bass_kernel_spmd
from concourse.ap import AP

F32 = mybir.dt.float32
BF16 = mybir.dt.bfloat16
ALU = mybir.AluOpType
AF = mybir.ActivationFunctionType
AX = mybir.AxisListType

NCORES = 8
D = 1024
HALF = 4096
NCH = 32
T1 = 256
CPT = T1 // 128
NT1 = HALF // T1
T2 = 512
NT2 = HALF // T2
DFF = 2816
NJB = DFF // 128
INW = 4360
EPS = 1e-6
C_AQ, C_AK, C_AV, C_MQ, C_MK, C_MV, C_MO, C_MI, C_MF, C_GA, C_GM = 0, 512, 640, 768, 1024, 1280, 1792, 2304, 2308, 2312, 3336


class Sched:
    ENG = ("pe", "dve", "act", "pool", "sp")

    def __init__(self, nc, es, n_dma=16):
        self.nc = nc
        self.semobj = {}
        self.cnt = {}
        self.base = {}
        for n in self.ENG:
            self.semobj[("e", n)] = es.enter_context(nc.semaphore("s_" + n))
            self.cnt[("e", n)] = 0
            self.base[("e", n)] = 0
        self.ndma = n_dma
        for i in range(n_dma):
            self.semobj[("d", i)] = es.enter_context(nc.semaphore("d%d" % i))
            self.cnt[("d", i)] = 0
        self.dnext2 = [0, 0]
        self.seen = {n: {} for n in self.ENG}
        self.lastw = {}
        self.readers = {}
        self.prog = {n: [] for n in self.ENG}

    def _wait(self, eng, sid, val):
        if val <= 0:
            return
        s = self.seen[eng]
        if s.get(sid, 0) >= val:
            return
        s[sid] = val
        self.prog[eng].append(("wait", sid, val))

    def _need(self, eng, reads, writes, is_dma):
        me = ("e", eng)
        need = {}

        def add(sid, v):
            if need.get(sid, 0) < v:
                need[sid] = v

        for k in reads:
            lw = self.lastw.get(k)
            if lw is not None:
                if lw[0] == me and eng == "pe" and not is_dma:
                    continue
                add(*lw)
        strict = is_dma or eng != "pe"
        for k in writes:
            lw = self.lastw.get(k)
            if lw is not None and (lw[0] != me or strict):
                add(*lw)
            for sid, v in self.readers.get(k, {}).items():
                if sid != me or strict:
                    add(sid, v)
        return need

    def op(self, eng, fn, reads=(), writes=()):
        if eng != "pe":
            extra = [("rd", k) for k in reads if k == "PT" or (isinstance(k, tuple) and k[0] in ("PS", "PY"))]
            if extra:
                writes = list(writes) + extra
        need = self._need(eng, reads, writes, False)
        for sid, v in need.items():
            self._wait(eng, sid, v)
        me = ("e", eng)
        self.cnt[me] += 1
        c = self.cnt[me]
        self.prog[eng].append(("op", fn, c))
        for k in reads:
            self.readers.setdefault(k, {})[me] = c
        for k in writes:
            self.lastw[k] = (me, c)
            self.readers[k] = {}

    def dma(self, q, out, in_, reads=(), writes=()):
        half = self.ndma // 2
        qi = 1 if q == "pool" else 0
        i = self.dnext2[qi] + qi * half
        self.dnext2[qi] = (self.dnext2[qi] + 1) % half
        sid = ("d", i)
        need = self._need(q, reads, writes, True)
        need[sid] = max(need.get(sid, 0), self.cnt[sid])
        for s, v in need.items():
            self._wait(q, s, v)
        self.cnt[sid] += 16
        c = self.cnt[sid]
        self.prog[q].append(("dma", out, in_, sid))
        for k in reads:
            self.readers.setdefault(k, {})[sid] = c
        for k in writes:
            self.lastw[k] = (sid, c)
            self.readers[k] = {}

    def barrier(self):
        for eng in self.ENG:
            for sid, v in self.cnt.items():
                if sid == ("e", eng):
                    continue
                self._wait(eng, sid, v)

    def emit(self):
        nc = self.nc
        prog = self.prog
        self.prog = {n: [] for n in self.ENG}
        waited = {("e", n): set() for n in self.ENG}
        for eng in self.ENG:
            for ent in prog[eng]:
                if ent[0] == "wait" and ent[1][0] == "e":
                    waited[ent[1]].add(ent[2])
        remap = {}
        for sid, vs in waited.items():
            b0 = self.base[sid]
            remap[sid] = {v: b0 + i + 1 for i, v in enumerate(sorted(vs))}
            self.base[sid] = b0 + len(vs)
        semobj = self.semobj
        self.n_inc = {n: len(waited[("e", n)]) for n in self.ENG}

        def run(eng, e):
            me = ("e", eng)
            for ent in prog[eng]:
                if ent[0] == "wait":
                    _, sid, v = ent
                    e.wait_ge(semobj[sid], remap[sid][v] if sid[0] == "e" else v)
                elif ent[0] == "op":
                    ins = ent[1](e)
                    if ent[2] in remap[me]:
                        ins.then_inc(semobj[me], 1)
                else:
                    _, out, in_, sid = ent
                    e.dma_start(out=out, in_=in_).then_inc(semobj[sid], 16)

        with nc.Block() as block:
            @block.tensor
            def _(e):
                run("pe", e)

            @block.vector
            def _(e):
                run("dve", e)

            @block.scalar
            def _(e):
                run("act", e)

            @block.gpsimd
            def _(e):
                run("pool", e)

            @block.sync
            def _(e):
                run("sp", e)


def bc(a, pos, n):
    dims = [list(d) for d in a.ap]
    dims.insert(pos, [0, n])
    return AP(a.tensor, a.offset, dims)


def build_program():
    nc = bass.Bass("TRN2", target_bir_lowering=False)

    def din(name, shape, dt=F32):
        return nc.dram_tensor(name, list(shape), dt, kind="ExternalInput").ap()

    xm = din("xm", [HALF, D])
    xp = din("xp", [HALF, D])
    w_in = din("w_in", [D, INW])
    w_ab = din("w_ab", [512, D])
    w_mb = din("w_mb", [512, D])
    w_o = din("w_o", [D, D])
    w_fi = din("w_fi", [D, 2 * DFF])
    w_fo = din("w_fo", [DFF, D])
    gains_d = din("gains", [128, 4, D])
    hng_d = din("hng", [128, 512])
    sinks_d = din("sinks", [128, 8])
    bif_d = din("bif", [128, 8])
    convw_d = din("convw", [128, 4, 4])
    convb_d = din("convb", [128, 4])
    flag_d = din("flag", [128, 1])
    cos_d = din("cosT", [128, NCH + 1, 32])
    sin_d = din("sinS", [128, NCH + 1, 64])
    mask_d = din("mask", [128, 2, 256], BF16)
    idb_d = din("idb", [128, 128], BF16)
    idf_d = din("idf", [128, 128])
    trif_d = din("trif", [128, 128])
    trib_d = din("trib", [128, 128], BF16)
    onesf_d = din("onesf", [128, 128])
    out = nc.dram_tensor("out", [HALF, D], F32, kind="ExternalOutput").ap()

    with ExitStack() as es:
        S = Sched(nc, es)
        PT = es.enter_context(nc.psum_tensor("PT", [128, 1024], BF16))
        PY = [es.enter_context(nc.psum_tensor("PY%d" % i, [128, 1024], F32)) for i in range(2)]
        PS = [es.enter_context(nc.psum_tensor("PS%d" % i, [128, 512], F32)) for i in range(3)]
        rr = [0]

        def nextP():
            i = rr[0]
            rr[0] = (i + 1) % 3
            return PS[i], ("PS", i)

        with ExitStack() as e1:
            def sb(name, shape, dt):
                return e1.enter_context(nc.sbuf_tensor(name, list(shape), dt))

            Wi = sb("Wi", [128, 8, INW], BF16)
            Wab = sb("Wab", [128, 4, D], BF16)
            Wmb = sb("Wmb", [128, 4, D], BF16)
            Wo = sb("Wo", [128, 8, D], BF16)
            cosT = sb("cosT_s", [128, NCH + 1, 32], F32)
            sinS = sb("sinS_s", [128, NCH + 1, 64], F32)
            gains = sb("gains_s", [128, 2, D], F32)
            hng = sb("hng_s", [128, 512], F32)
            sinks = sb("sinks_s", [128, 8], F32)
            bif = sb("bif_s", [128, 8], F32)
            convw = sb("convw_s", [128, 4, 4], F32)
            convb = sb("convb_s", [128, 4], F32)
            flag = sb("flag_s", [128, 1], F32)
            mask = sb("mask_s", [128, 2, 256], BF16)
            idb = sb("idb_s", [128, 128], BF16)
            idf = sb("idf_s", [128, 128], F32)
            trif = sb("trif_s", [128, 128], F32)
            trib = sb("trib_s", [128, 128], BF16)
            onesf = sb("onesf_s", [128, 128], F32)

            x_sb = [sb("x_sb%d" % i, [128, CPT, D], F32) for i in range(2)]
            h_sb = sb("h_sb", [128, D], BF16)
            junk = sb("junk", [128, D], BF16)
            hT = sb("hT", [128, 8, T1], BF16)
            scr = sb("scr", [128, 1280], F32)
            qk_r = sb("qk_r", [128, 640], BF16)
            qT = sb("qT", [128, 4, T1], BF16)
            kTb = sb("kTb", [128, (CPT + 1) * 128], BF16)
            vb = sb("vb", [128, CPT + 1, 2, 66], BF16)
            mv_aug = sb("mv_aug", [128, CPT, 4, 130], BF16)
            tmo = sb("tmo", [128, CPT, 512], BF16)
            gates_tm = sb("gates_tm", [128, CPT, 8], F32)
            conv_buf = sb("conv_buf", [128, 4, T1 + 3], F32)
            cacc = sb("cacc", [128, 4, T1], F32)
            stg = sb("stg", [128, 2, 1024], F32)
            ctanh = sb("ctanh", [128, 4, T1], BF16)
            mqkT = sb("mqkT", [128, 4, T1], BF16)
            k_tm = sb("k_tm", [128, CPT, 256], BF16)
            pexp = sb("pexp", [128, 4, 256], BF16)
            pTt = sb("pTt", [128, 2, 4, 128], BF16)
            attn_o = sb("attn_o", [128, 512], BF16)
            attn_oT = sb("attn_oT", [128, 4, T1], BF16)
            kw = sb("kw", [128, 4, 64], BF16)
            STw = sb("STw", [128, 4, 128], BF16)
            cellg = sb("cellg", [128, 4, 128], BF16)
            mo_out = sb("mo_out", [128, 512], BF16)
            mlstm_oT = sb("mlstm_oT", [128, 4, T1], BF16)
            Cst = sb("Cst", [128, 2, 130], F32)
            Chat = sb("Chat", [128, 2, 130], F32)
            Chat_z = sb("Chat_z", [128, 4, 130], BF16)
            tg = sb("tg", [128, 512], BF16)
            mergedT = sb("mergedT", [128, 8, T1], BF16)
            st = sb("st", [128, 64], F32)
            lfp = sb("lfp", [128, CPT, 4], F32)
            gb = sb("gb", [128, CPT, 8], F32)
            ab = sb("ab", [128, 2, CPT * 4], F32)
            d2 = sb("d2", [128, 2, CPT * 4], F32)
            ew = sb("ew", [128, 2, CPT * 4], F32)
            wq = sb("wq", [128, CPT * 4], F32)
            fbc = sb("fbc", [128, CPT * 4], F32)
            hp = sb("hp", [4, 32], F32)
            Rm = sb("Rm", [128, 2 * CPT, 4], F32)
            mst = sb("mst", [4, 1], F32)

            for dst, src, k in [(cosT, cos_d, "cos"), (sinS, sin_d, "sin"), (hng, hng_d, "hng"), (sinks, sinks_d, "sinks"),
                                (bif, bif_d, "bif"), (convw, convw_d, "convw"), (convb, convb_d, "convb"), (flag, flag_d, "flag"),
                                (mask, mask_d, "mask"), (idb, idb_d, "idb"), (idf, idf_d, "idf"), (trif, trif_d, "trif"),
                                (trib, trib_d, "trib"), (onesf, onesf_d, "onesf")]:
                S.dma("sp", dst[:], src, writes=[k])
            S.dma("sp", gains[:], gains_d[:, 0:2, :], writes=["gains"])

            stg_i = [0]
            pending = []

            def load_cast(dst, src, wkeys_, scale=1.0):
                def go():
                    sl = stg_i[0] % 2
                    stg_i[0] += 1
                    shp = list(dst.shape)
                    n = 1
                    for d_ in shp[1:]:
                        n *= d_
                    v = stg[:, sl, 0:n]
                    if len(shp) == 3:
                        v = v.rearrange("p (a b) -> p a b", a=shp[1])
                    S.dma("sp", v, src, writes=[("stg", sl)])
                    S.op("act", lambda e: e.activation(out=dst, in_=v, func=AF.Copy, scale=scale), reads=[("stg", sl)], writes=wkeys_)
                pending.append(go)

            def flush(n):
                for _ in range(min(n, len(pending))):
                    pending.pop(0)()

            w_in_v = w_in.rearrange("(kc p) n -> p kc n", p=128)
            wranges = [(C_MK, C_MV), (C_MV, C_MO), (C_MI, C_GA), (C_AK, C_MQ), (C_AQ, C_AK), (C_MO, C_MI), (C_MQ, C_MK),
                       (C_GA, C_GA + 512), (C_GA + 512, C_GM), (C_GM, C_GM + 512), (C_GM + 512, INW)]
            wkeys = {}
            for (a, b) in wranges:
                for c in range(a, b):
                    wkeys[c] = ("Wi", a)

            def load_w_in(rs):
                for (a, b) in rs:
                    g = max(1, min(4, 1024 // (b - a)))
                    for k0 in range(0, 8, g):
                        load_cast(Wi[:, k0:k0 + g, a:b], w_in_v[:, k0:k0 + g, a:b], [("Wi", a, k0 // 4)])

            def wk(col):
                k = wkeys[col]
                return [(k[0], k[1], 0), (k[0], k[1], 1)]

            load_w_in(wranges[:3])
            flush(len(pending))
            load_w_in(wranges[3:])
            for (dst, src, k) in [(Wab, w_ab, "Wab"), (Wmb, w_mb, "Wmb")]:
                sv = src.rearrange("(kc p) n -> p kc n", p=128)
                for kc in range(4):
                    load_cast(dst[:, kc, :], sv[:, kc, :], [k], scale=0.5)
            sv = w_o.rearrange("(kc p) n -> p kc n", p=128)
            for kc in range(8):
                load_cast(Wo[:, kc, :], sv[:, kc, :], ["Wo"])

            S.op("dve", lambda e: e.memset(Cst[:], 0.0), writes=["Cst"])
            S.op("dve", lambda e: e.memset(mst[:], 0.0), writes=["mst"])
            S.op("dve", lambda e: e.memset(Chat_z[:], 0.0), writes=["Chat_bf"])
            S.op("dve", lambda e: e.memset(Rm[:], 0.0), writes=["Rm"])
            S.op("dve", lambda e: e.memset(conv_buf[:], 0.0), writes=["convbuf"])
            S.op("dve", lambda e: e.memset(mv_aug[:, :, :, 128:130], 1.0), writes=["mv_aug0", "mv_aug1"])
            S.op("dve", lambda e: e.memset(vb[:, :, :, 64:66], 1.0), writes=["vb0", "vb1", "vb2"])
            S.op("dve", lambda e: e.memset(kTb[:, 0:128], 0.0), writes=["kT0"])
            S.op("dve", lambda e: e.memset(vb[:, 0, :, 0:64], 0.0), reads=["vb0"], writes=["vb0"])
            S.op("dve", lambda e: e.tensor_scalar(out=convw[:], in0=convw[:], scalar1=0.5, scalar2=None, op0=ALU.mult), reads=["convw"], writes=["convw"])
            S.op("dve", lambda e: e.tensor_scalar(out=convb[:], in0=convb[:], scalar1=0.5, scalar2=None, op0=ALU.mult), reads=["convb"], writes=["convb"])
            S.op("dve", lambda e: e.tensor_scalar(out=hng[:], in0=hng[:], scalar1=0.5, scalar2=None, op0=ALU.mult), reads=["hng"], writes=["hng"])

            def load_x(src, ti):
                slot = ti % 2
                for c in range(CPT):
                    r0 = (ti * CPT + c) * 128
                    S.dma("sp", x_sb[slot][:, c, :], src[r0:r0 + 128, :], writes=[("x", slot, c)])

            def rstd_from_ssq(col, n, inv_n, key):
                v = st[:, col:col + n]
                S.op("dve", lambda e: e.tensor_scalar(out=v, in0=v, scalar1=inv_n, scalar2=EPS, op0=ALU.mult, op1=ALU.add), reads=[key], writes=[key])
                S.op("act", lambda e: e.activation(out=v, in_=v, func=AF.Ln), reads=[key], writes=[key])
                S.op("act", lambda e: e.activation(out=v, in_=v, func=AF.Exp, scale=-0.5), reads=[key], writes=[key])

            def transpose_to(dst_ap, srcs, reads, writes, evac="act"):
                n = len(srcs)
                for i, s_ap in enumerate(srcs):
                    S.op("pe", lambda e, s_ap=s_ap, i=i: e.transpose(PT[:, i * 128:(i + 1) * 128], s_ap, idb[:]),
                         reads=list(reads) + ["idb"], writes=["PT"])
                src = PT[:, 0:n * 128]
                if evac == "act":
                    S.op("act", lambda e: e.activation(out=dst_ap, in_=src if len(dst_ap.shape) == 2 else src.rearrange("p (a b) -> p a b", b=128), func=AF.Copy),
                         reads=["PT"], writes=writes)
                else:
                    S.op("dve", lambda e: e.tensor_copy(out=dst_ap, in_=src if len(dst_ap.shape) == 2 else src.rearrange("p (a b) -> p a b", b=128)),
                         reads=["PT"], writes=writes)

            def mm_group(out_ap, pkey, pairs, reads):
                n = len(pairs)
                for i, (l, r) in enumerate(pairs):
                    S.op("pe", lambda e, l=l, r=r, i=i: e.matmul(out_ap, lhsT=l, rhs=r, start=(i == 0), stop=(i == n - 1)),
                         reads=reads, writes=[pkey])

            def mix_tile(ti, pre, last_pre):
                slot = ti % 2
                xs = x_sb[slot]
                main = not pre
                need_kv = main or last_pre
                S.op("dve", lambda e: e.memset(st[:, 0:8], 0.0), writes=["st_n"])
                for c in range(CPT):
                    S.op("act", lambda e, c=c: e.activation(out=junk[:], in_=xs[:, c, :], func=AF.Square, accum_out=st[:, c:c + 1]),
                         reads=[("x", slot, c), "st_n"], writes=["junk", "st_n"])
                rstd_from_ssq(0, CPT, 1.0 / D, "st_n")
                for c in range(CPT):
                    S.op("dve", lambda e, c=c: e.scalar_tensor_tensor(out=h_sb[:], in0=xs[:, c, :], scalar=st[:, c:c + 1], in1=gains[:, 0, :],
                                                                      op0=ALU.mult, op1=ALU.mult),
                         reads=[("x", slot, c), "st_n", "gains"], writes=["h_sb"])
                    transpose_to(hT[:, :, c * 128:(c + 1) * 128], [h_sb[:, k * 128:(k + 1) * 128] for k in range(8)],
                                 reads=["h_sb"], writes=[("hT", c)])
                hTk = [("hT", c) for c in range(CPT)]

                for c in range(CPT):
                    j = ti * CPT + c + 1
                    lhs = [hT[:, k, c * 128:(c + 1) * 128] for k in range(8)]
                    if need_kv:
                        tj = j if main else 0
                        store_kv = main or (c == CPT - 1)
                    if main:
                        P, pk = nextP()
                        mm_group(P[:, 0:512], pk, [(lhs[k], Wi[:, k, C_AQ:C_AQ + 512]) for k in range(8)], reads=[("hT", c)] + wk(C_AQ))
                        qv = P[:, 0:512].rearrange("p (h t d) -> p h t d", h=8, t=2)
                        t1 = scr[:, 0:512].rearrange("p (h t d) -> p h t d", h=8, t=2)
                        t2 = scr[:, 640:1152].rearrange("p (h t d) -> p h t d", h=8, t=2)
                        cs = bc(cosT[:, tj, :], 1, 2)
                        sn = sinS[:, tj, :].rearrange("p (t d) -> p t d", t=2)
                        S.op("dve", lambda e, qv=qv, t1=t1, cs=cs: e.tensor_tensor(out=t1, in0=qv, in1=bc(cs, 1, 8), op=ALU.mult),
                             reads=[pk, "cos"], writes=["scr_a"])
                        S.op("dve", lambda e, qv=qv, t2=t2, sn=sn: e.tensor_tensor(out=t2[:, :, 0, :], in0=qv[:, :, 1, :], in1=bc(sn[:, 0, :], 1, 8), op=ALU.mult),
                             reads=[pk, "sin"], writes=["scr_b"])
                        S.op("dve", lambda e, qv=qv, t2=t2, sn=sn: e.tensor_tensor(out=t2[:, :, 1, :], in0=qv[:, :, 0, :], in1=bc(sn[:, 1, :], 1, 8), op=ALU.mult),
                             reads=[pk, "sin"], writes=["scr_b"])
                        S.op("dve", lambda e: e.tensor_tensor(out=qk_r[:, 0:512].rearrange("p (g kv d) -> p kv g d", g=4, kv=2),
                                                               in0=scr[:, 0:512].rearrange("p (kv g d) -> p kv g d", kv=2, g=4),
                                                               in1=scr[:, 640:1152].rearrange("p (kv g d) -> p kv g d", kv=2, g=4), op=ALU.add),
                             reads=["scr_a", "scr_b"], writes=["qk_r"])
                    P, pk = nextP()
                    if need_kv:
                        mm_group(P[:, 0:256], pk, [(lhs[k], Wi[:, k, C_AK:C_AK + 256]) for k in range(8)], reads=[("hT", c)] + wk(C_AK))
                    mm_group(P[:, 256:264], pk, [(lhs[k], Wi[:, k, C_MI:C_MI + 8]) for k in range(8)], reads=[("hT", c)] + wk(C_MI))
                    S.op("act", lambda e, P=P, c=c: e.activation(out=gates_tm[:, c, :], in_=P[:, 256:264], func=AF.Copy), reads=[pk], writes=["gates_tm"])
                    if need_kv and store_kv:
                        slotk = c + 1 if main else 0
                        kv_ = P[:, 0:128].rearrange("p (h t d) -> p h t d", h=2, t=2)
                        t1 = scr[:, 512:640].rearrange("p (h t d) -> p h t d", h=2, t=2)
                        t2 = scr[:, 1152:1280].rearrange("p (h t d) -> p h t d", h=2, t=2)
                        cs = bc(cosT[:, tj, :], 1, 2)
                        sn = sinS[:, tj, :].rearrange("p (t d) -> p t d", t=2)
                        S.op("dve", lambda e, kv_=kv_, t1=t1, cs=cs: e.tensor_tensor(out=t1, in0=kv_, in1=bc(cs, 1, 2), op=ALU.mult),
                             reads=[pk, "cos"], writes=["scr_c"])
                        S.op("dve", lambda e, kv_=kv_, t2=t2, sn=sn: e.tensor_tensor(out=t2[:, :, 0, :], in0=kv_[:, :, 1, :], in1=bc(sn[:, 0, :], 1, 2), op=ALU.mult),
                             reads=[pk, "sin"], writes=["scr_d"])
                        S.op("dve", lambda e, kv_=kv_, t2=t2, sn=sn: e.tensor_tensor(out=t2[:, :, 1, :], in0=kv_[:, :, 0, :], in1=bc(sn[:, 1, :], 1, 2), op=ALU.mult),
                             reads=[pk, "sin"], writes=["scr_d"])
                        S.op("dve", lambda e: e.tensor_tensor(out=qk_r[:, 512:640], in0=scr[:, 512:640], in1=scr[:, 1152:1280], op=ALU.add),
                             reads=["scr_c", "scr_d"], writes=["qk_r_k"])
                        S.op("act", lambda e, P=P, slotk=slotk: e.activation(out=vb[:, slotk, :, 0:64], in_=P[:, 128:256].rearrange("p (h d) -> p h d", h=2), func=AF.Copy),
                             reads=[pk], writes=["vb%d" % slotk])
                        if main:
                            srcs = [qk_r[:, g * 128:(g + 1) * 128] for g in range(5)]
                            for i, s_ap in enumerate(srcs):
                                S.op("pe", lambda e, s_ap=s_ap, i=i: e.transpose(PT[:, i * 128:(i + 1) * 128], s_ap, idb[:]),
                                     reads=["qk_r", "qk_r_k", "idb"], writes=["PT"])
                            S.op("act", lambda e, c=c: e.activation(out=qT[:, :, c * 128:(c + 1) * 128], in_=PT[:, 0:512].rearrange("p (a b) -> p a b", b=128), func=AF.Copy),
                                 reads=["PT"], writes=[("qT", c)])
                            S.op("act", lambda e, slotk=slotk: e.activation(out=kTb[:, slotk * 128:(slotk + 1) * 128], in_=PT[:, 512:640], func=AF.Copy),
                                 reads=["PT"], writes=["kT%d" % slotk])
                        else:
                            S.op("pe", lambda e: e.transpose(PT[:, 0:128], qk_r[:, 512:640], idb[:]), reads=["qk_r_k", "idb"], writes=["PT"])
                            S.op("act", lambda e: e.activation(out=kTb[:, 0:128], in_=PT[:, 0:128], func=AF.Copy), reads=["PT"], writes=["kT0"])
                    P, pk = nextP()
                    mm_group(P[:, 0:512], pk, [(lhs[k], Wi[:, k, C_MV:C_MV + 512]) for k in range(8)], reads=[("hT", c)] + wk(C_MV))
                    S.op("act", lambda e, P=P, c=c: e.activation(out=mv_aug[:, c, :, 0:128], in_=P[:, 0:512].rearrange("p (h d) -> p h d", h=4), func=AF.Copy),
                         reads=[pk], writes=["mv_aug%d" % c])
                    if main:
                        P, pk = nextP()
                        mm_group(P[:, 0:512], pk, [(lhs[k], Wi[:, k, C_MO:C_MO + 512]) for k in range(8)], reads=[("hT", c)] + wk(C_MO))
                        S.op("act", lambda e, P=P, c=c: e.activation(out=tmo[:, c, :], in_=P[:, 0:512], func=AF.Tanh, scale=0.5), reads=[pk], writes=[("tmo", c)])

                groups = [0, 1, 2, 3] if (main or last_pre) else [2, 3]
                for m in groups:
                    P, pk = nextP()
                    mm_group(P[:, 0:T1], pk, [(Wi[:, k, C_MQ + m * 128:C_MQ + (m + 1) * 128], hT[:, k, :]) for k in range(8)], reads=hTk + wk(C_MQ + m * 128))
                    S.op("act", lambda e, P=P, m=m: e.activation(out=conv_buf[:, m, 3:3 + T1], in_=P[:, 0:T1], func=AF.Copy), reads=[pk, "convbuf"], writes=[("cb", m)])
                    S.op("dve", lambda e, m=m: e.tensor_scalar(out=cacc[:, m, :], in0=conv_buf[:, m, 0:T1], scalar1=convw[:, m, 0:1], scalar2=convb[:, m:m + 1],
                                                                op0=ALU.mult, op1=ALU.add), reads=[("cb", m), "convw", "convb", "convbuf"], writes=[("cacc", m)])
                    for jt in range(1, 4):
                        S.op("dve", lambda e, m=m, jt=jt: e.scalar_tensor_tensor(out=cacc[:, m, :], in0=conv_buf[:, m, jt:jt + T1], scalar=convw[:, m, jt:jt + 1],
                                                                                in1=cacc[:, m, :], op0=ALU.mult, op1=ALU.add),
                             reads=[("cb", m), "convw", ("cacc", m)], writes=[("cacc", m)])
                    S.op("act", lambda e, m=m: e.activation(out=conv_buf[:, m, 0:3], in_=conv_buf[:, m, T1:T1 + 3], func=AF.Copy), reads=[("cb", m)], writes=[("cb", m)])
                    S.op("act", lambda e, m=m: e.activation(out=ctanh[:, m, :], in_=cacc[:, m, :], func=AF.Tanh), reads=[("cacc", m)], writes=[("ctanh", m)])
                    S.op("dve", lambda e, m=m: e.scalar_tensor_tensor(out=mqkT[:, m, :], in0=ctanh[:, m, :], scalar=1.0, in1=cacc[:, m, :], op0=ALU.add, op1=ALU.mult),
                         reads=[("ctanh", m), ("cacc", m)], writes=[("mqkT", m)])
                for c in range(CPT):
                    transpose_to(k_tm[:, c, :], [mqkT[:, 2, c * 128:(c + 1) * 128], mqkT[:, 3, c * 128:(c + 1) * 128]],
                                 reads=[("mqkT", 2), ("mqkT", 3)], writes=[("k_tm", c)], evac="dve")

                NG = CPT * 4
                S.op("dve", lambda e: e.tensor_tensor(out=gb[:], in0=gates_tm[:], in1=bc(bif[:, 0:8], 1, CPT), op=ALU.add), reads=["gates_tm", "bif"], writes=["gb"])
                S.op("act", lambda e: e.activation(out=lfp[:], in_=gb[:, :, 4:8], func=AF.Exp, scale=-1.0), reads=["gb"], writes=["lfp"])
                S.op("act", lambda e: e.activation(out=lfp[:], in_=lfp[:], func=AF.Ln, bias=1.0), reads=["lfp"], writes=["lfp"])
                P, pk = nextP()
                S.op("pe", lambda e, P=P: e.matmul(P[:, 0:NG], lhsT=trif[:], rhs=lfp[:].rearrange("p c h -> p (c h)"), start=True, stop=True),
                     reads=["lfp", "trif"], writes=[pk])
                abv = ab[:].rearrange("p k (c h) -> p k c h", c=CPT)
                S.op("dve", lambda e, P=P: e.tensor_tensor(out=abv[:, 0, :, :], in0=P[:, 0:NG].rearrange("p (c h) -> p c h", c=CPT), in1=gb[:, :, 0:4], op=ALU.add),
                     reads=[pk, "gb"], writes=["ab"])
                S.op("dve", lambda e, P=P: e.tensor_copy(out=ab[:, 1, :], in_=P[:, 0:NG]), reads=[pk], writes=["ab"])
                P2, pk2 = nextP()
                for c in range(CPT):
                    S.op("pe", lambda e, P2=P2, c=c: e.matmul(P2[0:4, c * 128:(c + 1) * 128], lhsT=ab[:, 0, c * 4:(c + 1) * 4], rhs=idf[:], start=True, stop=True),
                         reads=["ab", "idf"], writes=[pk2])
                for c in range(CPT):
                    S.op("pe", lambda e, P2=P2, c=c: e.matmul(P2[0:4, 256 + c * 128:256 + (c + 1) * 128], lhsT=ab[:, 1, c * 4:(c + 1) * 4], rhs=idf[:], start=True, stop=True),
                         reads=["ab", "idf"], writes=[pk2])
                S.op("dve", lambda e, P2=P2: e.reduce_max(out=hp[:, 0:CPT], in_=P2[0:4, 0:CPT * 128].rearrange("p (c t) -> p c t", c=CPT), axis=AX.X),
                     reads=[pk2], writes=["hp"])
                S.op("dve", lambda e, P2=P2: e.tensor_copy(out=hp[:, 4:4 + CPT], in_=P2[0:4, 256:256 + CPT * 128].rearrange("p (c t) -> p c t", c=CPT)[:, :, 127]), reads=[pk2], writes=["hp"])
                for c in range(CPT):
                    S.op("dve", lambda e, c=c: e.tensor_tensor(out=hp[:, 8 + c:9 + c], in0=mst[:], in1=hp[:, c:c + 1], op=ALU.max), reads=["hp", "mst"], writes=["hp"])
                    S.op("dve", lambda e, c=c: e.tensor_tensor(out=hp[:, 12 + c:13 + c], in0=mst[:], in1=hp[:, 8 + c:9 + c], op=ALU.subtract), reads=["hp", "mst"], writes=["hp"])
                    S.op("dve", lambda e, c=c: e.tensor_tensor(out=mst[:], in0=hp[:, 8 + c:9 + c], in1=hp[:, 4 + c:5 + c], op=ALU.subtract), reads=["hp"], writes=["mst"])
                S.op("act", lambda e: e.activation(out=hp[:, 12:12 + CPT], in_=hp[:, 12:12 + CPT], func=AF.Exp), reads=["hp"], writes=["hp"])
                hp_p = hp[:].ap[0][0]
                muf = AP(hp[:].tensor, hp[:].offset + 8, [[hp_p, 4], [4, 2], [1, CPT], [0, 4]])
                idf_p = idf[:].ap[0][0]
                i4 = AP(idf[:].tensor, idf[:].offset, [[idf_p, 4], [0, 2], [0, CPT], [1, 4]])
                S.op("dve", lambda e: e.tensor_tensor(out=Rm[0:4, :, :].rearrange("p (k c) h -> p k c h", k=2), in0=muf, in1=i4, op=ALU.mult), reads=["hp", "idf"], writes=["Rm"])
                P3, pk3 = nextP()
                S.op("pe", lambda e, P3=P3: e.matmul(P3[:, 0:2 * NG], lhsT=onesf[:, :], rhs=Rm[:].rearrange("p a h -> p (a h)"), start=True, stop=True),
                     reads=["Rm", "onesf"], writes=[pk3])
                S.op("dve", lambda e, P3=P3: e.tensor_tensor(out=d2[:], in0=ab[:], in1=bc(P3[:, 0:NG], 1, 2), op=ALU.subtract), reads=[pk3, "ab"], writes=["d2"])
                S.op("dve", lambda e, P3=P3: e.tensor_copy(out=fbc[:], in_=P3[:, NG:2 * NG]), reads=[pk3], writes=["fbc"])
                S.op("act", lambda e: e.activation(out=ew[:], in_=d2[:], func=AF.Exp), reads=["d2"], writes=["ew"])
                S.op("dve", lambda e: e.tensor_scalar(out=wq[:], in0=ew[:, 0, :], scalar1=0.125, scalar2=None, op0=ALU.mult), reads=["ew"], writes=["wq"])

                for c in range(CPT):
                    jg = ti * CPT + c
                    if main and _DBG.get("attn", 1):
                        for kvg in range(2):
                            PYt, pyk = PY[kvg], ("PY", kvg)
                            rows = slice(kvg * 64, (kvg + 1) * 64)
                            for g in range(4):
                                S.op("pe", lambda e, g=g, PYt=PYt, rows=rows, c=c: e.matmul(PYt[:, g * 256:(g + 1) * 256], lhsT=qT[rows, g, c * 128:(c + 1) * 128],
                                                                                          rhs=kTb[rows, c * 128:(c + 2) * 128], start=True, stop=True),
                                     reads=[("qT", c), "kT%d" % c, "kT%d" % (c + 1)], writes=[pyk])
                            sc3 = PYt[:, :].rearrange("p (g k) -> p g k", g=4)
                            o = 16 + kvg * 16
                            S.op("dve", lambda e, PYt=PYt, o=o: e.reduce_max(out=st[:, o:o + 1], in_=PYt[:, :], axis=AX.X), reads=[pyk], writes=["st_a%d" % kvg])
                            S.op("dve", lambda e, o=o: e.tensor_scalar(out=st[:, o + 4:o + 8], in0=bc(st[:, o:o + 1], 1, 4)[:, :, 0] if False else AP(st[:].tensor, st[:].offset + o, [[st[:].ap[0][0], 128], [0, 4]]),
                                                                     scalar1=-0.125, scalar2=None, op0=ALU.mult),
                                 reads=["st_a%d" % kvg], writes=["st_a%d" % kvg])
                            for bk in range(2):
                                S.op("act", lambda e, bk=bk, PYt=PYt, o=o: e.activation(out=pexp[:, 2 * bk:2 * bk + 2, :], in_=PYt[:, bk * 512:(bk + 1) * 512].rearrange("p (g k) -> p g k", g=2),
                                                                                        func=AF.Exp, scale=0.125, bias=st[:, o + 4:o + 5]),
                                     reads=[pyk, "st_a%d" % kvg], writes=["pexp"])
                            mi = 1 if jg == 0 else 0
                            S.op("dve", lambda e, mi=mi: e.tensor_tensor(out=pexp[:], in0=pexp[:], in1=bc(mask[:, mi, :], 1, 4), op=ALU.mult),
                                 reads=["pexp", "mask"], writes=["pexp"])
                            for kb in range(2):
                                for g in range(4):
                                    i = kb * 4 + g
                                    S.op("pe", lambda e, g=g, kb=kb, i=i: e.transpose(PT[:, i * 128:(i + 1) * 128], pexp[:, g, kb * 128:(kb + 1) * 128], idb[:]),
                                         reads=["pexp", "idb"], writes=["PT"])
                            S.op("act", lambda e: e.activation(out=pTt[:].rearrange("p a g q -> p (a g q)"), in_=PT[:, 0:1024], func=AF.Copy), reads=["PT"], writes=["pTt"])
                            P, pk = nextP()
                            for g in range(4):
                                for kb in range(2):
                                    S.op("pe", lambda e, g=g, kb=kb, P=P, kvg=kvg, c=c: e.matmul(P[:, g * 65:(g + 1) * 65], lhsT=pTt[:, kb, g, :], rhs=vb[:, c + kb, kvg, 0:65],
                                                                                              start=(kb == 0), stop=(kb == 1)),
                                         reads=["pTt", "vb%d" % (c + kb)], writes=[pk])
                            S.op("dve", lambda e, o=o, kvg=kvg: e.tensor_tensor(out=st[:, o + 8:o + 12], in0=st[:, o + 4:o + 8], in1=sinks[:, kvg * 4:(kvg + 1) * 4], op=ALU.add),
                                 reads=["st_a%d" % kvg, "sinks"], writes=["st_a%d" % kvg])
                            S.op("act", lambda e, o=o: e.activation(out=st[:, o + 8:o + 12], in_=st[:, o + 8:o + 12], func=AF.Exp), reads=["st_a%d" % kvg], writes=["st_a%d" % kvg])
                            o3 = P[:, 0:260].rearrange("p (g d) -> p g d", g=4)
                            S.op("dve", lambda e, o=o, o3=o3: e.tensor_tensor(out=st[:, o + 8:o + 12], in0=o3[:, :, 64], in1=st[:, o + 8:o + 12], op=ALU.add),
                                 reads=[pk, "st_a%d" % kvg], writes=["st_a%d" % kvg])
                            S.op("dve", lambda e, o=o: e.reciprocal(out=st[:, o + 12:o + 16], in_=st[:, o + 8:o + 12]), reads=["st_a%d" % kvg], writes=["st_a%d" % kvg])
                            S.op("dve", lambda e, o=o, o3=o3, kvg=kvg: e.tensor_tensor(out=attn_o[:, kvg * 256:(kvg + 1) * 256].rearrange("p (g d) -> p g d", g=4),
                                                                                     in0=o3[:, :, 0:64], in1=bc(st[:, o + 12:o + 16], 2, 64), op=ALU.mult),
                                 reads=[pk, "st_a%d" % kvg], writes=["attn_o"])
                        transpose_to(attn_oT[:, :, c * 128:(c + 1) * 128], [attn_o[:, k * 128:(k + 1) * 128] for k in range(4)],
                                     reads=["attn_o"], writes=[("attn_oT", c)])

                    kt3 = k_tm[:, c, :].rearrange("p (h d) -> p h d", h=4)
                    S.op("dve", lambda e, kt3=kt3, c=c: e.tensor_tensor(out=kw[:], in0=kt3, in1=bc(wq[:, c * 4:(c + 1) * 4], 2, 64), op=ALU.mult),
                         reads=[("k_tm", c), "wq"], writes=["kw"])
                    fb_p = fbc[:].ap[0][0]
                    for par in range(2):
                        rows = slice(par * 64, (par + 1) * 64)
                        fsel = AP(fbc[:].tensor, fbc[:].offset + par * 64 * fb_p + c * 4 + par, [[fb_p, 64], [2, 2], [0, 130]])
                        S.op("dve", lambda e, rows=rows, fsel=fsel: e.tensor_tensor(out=Chat[rows, :, :], in0=Cst[rows, :, :], in1=fsel, op=ALU.mult),
                             reads=["Cst", "fbc"], writes=["Chat"])
                    if main:
                        cz_p = Chat_z[:].ap[0][0]
                        for par in range(2):
                            rows = slice(par * 64, (par + 1) * 64)
                            czsel = AP(Chat_z[:].tensor, Chat_z[:].offset + par * 64 * cz_p + par * 130, [[cz_p, 64], [260, 2], [1, 130]])
                            S.op("act", lambda e, rows=rows, czsel=czsel: e.activation(out=czsel, in_=Chat[rows, :, :], func=AF.Copy), reads=["Chat"], writes=["Chat_bf"])
                    PD, pdk = PY[1], ("PY", 1)
                    for h in range(4):
                        pr = h // 2
                        S.op("pe", lambda e, h=h, pr=pr, c=c, PD=PD: e.matmul(PD[:, h * 256:h * 256 + 129], lhsT=kw[:, 2 * pr:2 * pr + 2, :].rearrange("p a d -> p (a d)"),
                                                                           rhs=mv_aug[:, c, h, 0:129], start=True, stop=True),
                             reads=["kw", "mv_aug%d" % c], writes=[pdk])
                    mlvl = _DBG.get("mlm", 9)
                    mm_main = main and mlvl
                    if mm_main:
                        Pp = [nextP(), nextP()]
                        for h in range(4):
                            rows = slice((h % 2) * 64, (h % 2) * 64 + 64)
                            P, pk = Pp[h % 2]
                            S.op("pe", lambda e, h=h, rows=rows, P=P, c=c: e.matmul(P[:, (h // 2) * 128:(h // 2 + 1) * 128], lhsT=mqkT[rows, 2 + h // 2, c * 128:(c + 1) * 128],
                                                                                  rhs=mqkT[rows, h // 2, c * 128:(c + 1) * 128], start=True, stop=True),
                                 reads=[("mqkT", 0), ("mqkT", 1), ("mqkT", 2), ("mqkT", 3)], writes=[pk])
                        for h in range(4):
                            P, pk = Pp[h % 2]
                            S.op("dve", lambda e, h=h, P=P, c=c: e.scalar_tensor_tensor(out=STw[:, h, :], in0=P[:, (h // 2) * 128:(h // 2 + 1) * 128], scalar=wq[:, c * 4 + h:c * 4 + h + 1],
                                                                                       in1=trib[:], op0=ALU.mult, op1=ALU.mult),
                                 reads=[pk, "wq", "trib"], writes=["STw"])
                        PN, pnk = PY[0], ("PY", 0)
                        for h in (range(4) if mlvl >= 2 else []):
                            rows = slice((h % 2) * 64, (h % 2) * 64 + 64)
                            S.op("pe", lambda e, h=h, PN=PN, c=c: e.matmul(PN[:, h * 256:h * 256 + 129], lhsT=STw[:, h, :], rhs=mv_aug[:, c, h, 0:129], start=True, stop=False),
                                 reads=["STw", "mv_aug%d" % c], writes=[pnk])
                            S.op("pe", lambda e, h=h, PN=PN, c=c: e.matmul(PN[:, h * 256:h * 256 + 129], lhsT=mqkT[:, h // 2, c * 128:(c + 1) * 128],
                                                                         rhs=Chat_z[:, h, 0:129], start=False, stop=True),
                                 reads=[("mqkT", 0), ("mqkT", 1), "Chat_bf"], writes=[pnk])
                    PD3 = PD[:, :].rearrange("p (h x) -> p h x", h=4)
                    for par in range(2):
                        rows = slice(par * 64, (par + 1) * 64)
                        pd_p = PD[:, :].ap[0][0]
                        dsel = AP(PD[:, :].tensor, PD[:, :].offset + par * 64 * pd_p + par * 256, [[pd_p, 64], [512, 2], [1, 129]])
                        S.op("dve", lambda e, rows=rows, dsel=dsel: e.tensor_tensor(out=Cst[rows, :, 0:129], in0=Chat[rows, :, 0:129], in1=dsel, op=ALU.add),
                             reads=[pdk, "Chat"], writes=["Cst"])
                    if mm_main and mlvl >= 3:
                        PN3 = PN[:, :].rearrange("p (h x) -> p h x", h=4)
                        o = 48
                        S.op("dve", lambda e, PN3=PN3: e.tensor_copy(out=st[:, o + 12:o + 16], in_=PN3[:, :, 128]), reads=[pnk], writes=["st_m"])
                        S.op("dve", lambda e: e.scalar_tensor_tensor(out=st[:, o:o + 4], in0=st[:, o + 12:o + 16], scalar=-1.0, in1=st[:, o + 12:o + 16], op0=ALU.mult, op1=ALU.max),
                             reads=["st_m"], writes=["st_m"])
                        S.op("dve", lambda e, c=c: e.tensor_tensor(out=st[:, o:o + 4], in0=st[:, o:o + 4], in1=ew[:, 1, c * 4:(c + 1) * 4], op=ALU.max),
                             reads=["st_m", "ew"], writes=["st_m"])
                        S.op("dve", lambda e: e.reciprocal(out=st[:, o:o + 4], in_=st[:, o:o + 4]), reads=["st_m"], writes=["st_m"])
                        S.op("dve", lambda e: e.memset(st[:, o + 4:o + 8], 0.0), reads=["st_m"], writes=["st_m"])
                        for h in range(4):
                            S.op("act", lambda e, h=h, PN3=PN3: e.activation(out=junk[:, 0:128], in_=PN3[:, h, 0:128], func=AF.Square,
                                                                            accum_out=st[:, o + 4 + h:o + 5 + h]),
                                 reads=[pnk, "st_m"], writes=["junk", "st_m"])
                        S.op("dve", lambda e: e.tensor_tensor(out=st[:, o + 4:o + 8], in0=st[:, o + 4:o + 8], in1=st[:, o:o + 4], op=ALU.mult), reads=["st_m"], writes=["st_m"])
                        S.op("dve", lambda e: e.tensor_tensor(out=st[:, o + 4:o + 8], in0=st[:, o + 4:o + 8], in1=st[:, o:o + 4], op=ALU.mult), reads=["st_m"], writes=["st_m"])
                        rstd_from_ssq(o + 4, 4, 1.0 / 128, "st_m")
                        S.op("dve", lambda e: e.tensor_tensor(out=st[:, o + 8:o + 12], in0=st[:, o:o + 4], in1=st[:, o + 4:o + 8], op=ALU.mult), reads=["st_m"], writes=["st_m"])
                        for h in range(4):
                            S.op("dve", lambda e, h=h, PN3=PN3: e.scalar_tensor_tensor(out=cellg[:, h, :], in0=PN3[:, h, 0:128], scalar=st[:, o + 8 + h:o + 9 + h],
                                                                                      in1=hng[:, h * 128:(h + 1) * 128], op0=ALU.mult, op1=ALU.mult),
                                 reads=[pnk, "st_m", "hng"], writes=["cellg"])
                        S.op("dve", lambda e, c=c: e.scalar_tensor_tensor(out=mo_out[:], in0=tmo[:, c, :], scalar=1.0, in1=cellg[:].rearrange("p h d -> p (h d)"),
                                                                         op0=ALU.add, op1=ALU.mult),
                             reads=[("tmo", c), "cellg"], writes=["mo_out"])
                        if mlvl >= 4:
                            transpose_to(mlstm_oT[:, :, c * 128:(c + 1) * 128], [mo_out[:, k * 128:(k + 1) * 128] for k in range(4)],
                                         reads=["mo_out"], writes=[("mlstm_oT", c)])

                if not main or not _DBG.get("merge", 1):
                    return
                S.op("act", lambda e: e.activation(out=kTb[:, 0:128], in_=kTb[:, CPT * 128:(CPT + 1) * 128], func=AF.Copy), reads=["kT%d" % CPT], writes=["kT0"])
                S.op("act", lambda e: e.activation(out=vb[:, 0, :, :], in_=vb[:, CPT, :, :], func=AF.Copy), reads=["vb%d" % CPT], writes=["vb0"])

                aoT = [("attn_oT", c) for c in range(CPT)]
                moT = [("mlstm_oT", c) for c in range(CPT)]
                for ob in range(8):
                    X, xk = nextP()
                    mm_group(X[:, 0:T1], xk, [(Wi[:, k, C_GA + ob * 128:C_GA + (ob + 1) * 128], hT[:, k, :]) for k in range(8)], reads=hTk + wk(C_GA + ob * 128))
                    mm_group(X[:, T1:2 * T1], xk, [(Wi[:, k, C_GM + ob * 128:C_GM + (ob + 1) * 128], hT[:, k, :]) for k in range(8)], reads=hTk + wk(C_GM + ob * 128))
                    S.op("act", lambda e, X=X: e.activation(out=tg[:, 0:2 * T1], in_=X[:, 0:2 * T1], func=AF.Tanh, scale=0.5), reads=[xk], writes=["tg"])
                    Y, yk = nextP()
                    mm_group(Y[:, 0:T1], yk, [(Wab[:, k, ob * 128:(ob + 1) * 128], attn_oT[:, k, :]) for k in range(4)], reads=aoT + ["Wab"])
                    mm_group(Y[:, T1:2 * T1], yk, [(Wmb[:, k, ob * 128:(ob + 1) * 128], mlstm_oT[:, k, :]) for k in range(4)], reads=moT + ["Wmb"])
                    S.op("dve", lambda e, Y=Y: e.scalar_tensor_tensor(out=scr[:, 0:2 * T1], in0=tg[:, 0:2 * T1], scalar=1.0, in1=Y[:, 0:2 * T1], op0=ALU.add, op1=ALU.mult),
                         reads=["tg", yk], writes=["scr_a"])
                    S.op("dve", lambda e, ob=ob: e.tensor_tensor(out=mergedT[:, ob, :], in0=scr[:, 0:T1], in1=scr[:, T1:2 * T1], op=ALU.add),
                         reads=["scr_a"], writes=[("mergedT", ob)])
                mk_ = [("mergedT", ob) for ob in range(8)]
                S.op("dve", lambda e: e.memset(st[:, 8:16], 0.0), writes=["st_o"])
                for c in range(CPT):
                    PYt, pyk = PY[c % 2], ("PY", c % 2)
                    for nh in range(2):
                        mm_group(PYt[:, nh * 512:(nh + 1) * 512], pyk, [(mergedT[:, k, c * 128:(c + 1) * 128], Wo[:, k, nh * 512:(nh + 1) * 512]) for k in range(8)],
                                 reads=mk_ + ["Wo"])
                    for nh in range(2):
                        S.op("act", lambda e, PYt=PYt, c=c, nh=nh: e.activation(out=junk[:, 0:512], in_=PYt[:, nh * 512:(nh + 1) * 512], func=AF.Square,
                                                                               accum_out=st[:, 8 + 2 * c + nh:9 + 2 * c + nh]),
                             reads=[pyk, "st_o"], writes=["junk", "st_o"])
                    S.op("dve", lambda e, c=c: e.tensor_tensor(out=st[:, 12 + c:13 + c], in0=st[:, 8 + 2 * c:9 + 2 * c], in1=st[:, 9 + 2 * c:10 + 2 * c], op=ALU.add),
                         reads=["st_o"], writes=["st_o"])
                    S.op("dve", lambda e, c=c: e.tensor_scalar(out=st[:, 12 + c:13 + c], in0=st[:, 12 + c:13 + c], scalar1=1.0 / D, scalar2=EPS, op0=ALU.mult, op1=ALU.add),
                         reads=["st_o"], writes=["st_o"])
                    S.op("act", lambda e, c=c: e.activation(out=st[:, 12 + c:13 + c], in_=st[:, 12 + c:13 + c], func=AF.Ln), reads=["st_o"], writes=["st_o"])
                    S.op("act", lambda e, c=c: e.activation(out=st[:, 12 + c:13 + c], in_=st[:, 12 + c:13 + c], func=AF.Exp, scale=-0.5), reads=["st_o"], writes=["st_o"])
                    S.op("dve", lambda e, PYt=PYt, c=c: e.scalar_tensor_tensor(out=scr[:, 0:D], in0=PYt[:, :], scalar=st[:, 12 + c:13 + c], in1=gains[:, 1, :],
                                                                              op0=ALU.mult, op1=ALU.mult),
                         reads=[pyk, "st_o", "gains"], writes=["scr_a", "scr_b", "scr_c"])
                    S.op("dve", lambda e, c=c: e.tensor_tensor(out=xs[:, c, :], in0=xs[:, c, :], in1=scr[:, 0:D], op=ALU.add),
                         reads=[("x", slot, c), "scr_a", "scr_b", "scr_c"], writes=[("x", slot, c)])
                    r0 = (ti * CPT + c) * 128
                    S.dma("sp", out[r0:r0 + 128, :], xs[:, c, :], reads=[("x", slot, c)], writes=[("out", r0)])

            load_x(xp, 0)
            for ti in range(_DBG["n_pre"]):
                if ti + 1 < _DBG["n_pre"]:
                    load_x(xp, ti + 1)
                else:
                    load_x(xm, 0)
                flush(len(pending) if ti + 2 >= _DBG["n_pre"] else 8)
                mix_tile(ti, True, ti == _DBG["n_pre"] - 1)
            S.op("dve", lambda e: e.tensor_scalar(out=Cst[:], in0=Cst[:], scalar1=flag[:, 0:1], scalar2=None, op0=ALU.mult), reads=["Cst", "flag"], writes=["Cst"])
            S.op("dve", lambda e: e.tensor_scalar(out=mst[:], in0=mst[:], scalar1=flag[0:4, 0:1], scalar2=None, op0=ALU.mult), reads=["mst", "flag"], writes=["mst"])
            for ti in range(_DBG["n_main"]):
                if ti + 1 < _DBG["n_main"]:
                    load_x(xm, ti + 1)
                mix_tile(ti, False, False)
            S.barrier()
            S.emit()

        with ExitStack() as e2:
            def sb2(name, shape, dt):
                return e2.enter_context(nc.sbuf_tensor(name, list(shape), dt))

            Wfi = sb2("Wfi", [128, 8, 2 * DFF], BF16)
            Wfo = sb2("Wfo", [128, NJB, D], BF16)
            gains2 = sb2("gains2", [128, 2, D], F32)
            idb2 = sb2("idb2", [128, 128], BF16)
            x1 = sb2("x1", [128, T2 // 128, D], F32)
            h2 = sb2("h2", [128, D], BF16)
            junk2 = sb2("junk2", [128, D], BF16)
            h2T = sb2("h2T", [128, 8, T2], BF16)
            actT = sb2("actT", [128, NJB, T2], BF16)
            sg = [sb2("sg%d" % i, [128, T2], BF16) for i in range(2)]
            r2 = sb2("r2", [128, D], F32)
            st2 = sb2("st2", [128, 32], F32)

            S.dma("sp", idb2[:], idb_d, writes=["idb2"])
            S.dma("sp", gains2[:], gains_d[:, 2:4, :], writes=["gains2"])
            w_fi_v = w_fi.rearrange("(kc p) n -> p kc n", p=128)
            w_fo_v = w_fo.rearrange("(jb p) n -> p jb n", p=128)
            stg2 = sb2("stg2", [128, 2, 1024], F32)
            stg2_i = [0]
            pending2 = []

            def load_cast2(dst, src, wkeys_):
                def go():
                    sl = stg2_i[0] % 2
                    stg2_i[0] += 1
                    shp = list(dst.shape)
                    v = stg2[:, sl, :]
                    if len(shp) == 3:
                        v = v.rearrange("p (a b) -> p a b", a=shp[1])
                    eng = "act" if (stg2_i[0] % 2) else "dve"
                    S.dma("sp", v, src, writes=[("stg2", sl)])
                    if eng == "act":
                        S.op("act", lambda e: e.activation(out=dst, in_=v, func=AF.Copy), reads=[("stg2", sl)], writes=wkeys_)
                    else:
                        S.op("dve", lambda e: e.tensor_copy(out=dst, in_=v), reads=[("stg2", sl)], writes=wkeys_)
                pending2.append(go)

            for jb in range(NJB):
                for half in range(2):
                    c0 = half * DFF + jb * 128
                    load_cast2(Wfi[:, :, c0:c0 + 128], w_fi_v[:, :, c0:c0 + 128], [("Wfi", half, jb)])
            for jb in range(NJB):
                load_cast2(Wfo[:, jb, :], w_fo_v[:, jb, :], [("Wfo", jb)])

            NS = T2 // 128
            for ti in range(_DBG["p2"]):
                for c in range(NS):
                    r0 = (ti * NS + c) * 128
                    S.dma("sp", x1[:, c, :], out[r0:r0 + 128, :], reads=[("out", r0)], writes=[("x1", c)])
                if ti == 0:
                    while pending2:
                        pending2.pop(0)()
                S.op("dve", lambda e: e.memset(st2[:, 0:8], 0.0), writes=["st2"])
                for c in range(NS):
                    S.op("act", lambda e, c=c: e.activation(out=junk2[:], in_=x1[:, c, :], func=AF.Square, accum_out=st2[:, c:c + 1]),
                         reads=[("x1", c), "st2"], writes=["junk2", "st2"])
                S.op("dve", lambda e: e.tensor_scalar(out=st2[:, 0:NS], in0=st2[:, 0:NS], scalar1=1.0 / D, scalar2=EPS, op0=ALU.mult, op1=ALU.add), reads=["st2"], writes=["st2"])
                S.op("act", lambda e: e.activation(out=st2[:, 0:NS], in_=st2[:, 0:NS], func=AF.Ln), reads=["st2"], writes=["st2"])
                S.op("act", lambda e: e.activation(out=st2[:, 0:NS], in_=st2[:, 0:NS], func=AF.Exp, scale=-0.5), reads=["st2"], writes=["st2"])
                for c in range(NS):
                    S.op("dve", lambda e, c=c: e.scalar_tensor_tensor(out=h2[:], in0=x1[:, c, :], scalar=st2[:, c:c + 1], in1=gains2[:, 0, :], op0=ALU.mult, op1=ALU.mult),
                         reads=[("x1", c), "st2", "gains2"], writes=["h2"])
                    for k in range(8):
                        S.op("pe", lambda e, k=k: e.transpose(PT[:, k * 128:(k + 1) * 128], h2[:, k * 128:(k + 1) * 128], idb2[:]), reads=["h2", "idb2"], writes=["PT"])
                    S.op("act", lambda e, c=c: e.activation(out=h2T[:, :, c * 128:(c + 1) * 128], in_=PT[:, 0:1024].rearrange("p (a b) -> p a b", b=128), func=AF.Copy),
                         reads=["PT"], writes=[("h2T", c)])
                h2k = [("h2T", c) for c in range(NS)]
                for jb in range(NJB):
                    G, gk = nextP()
                    mm_group(G[:, 0:T2], gk, [(Wfi[:, k, jb * 128:(jb + 1) * 128], h2T[:, k, :]) for k in range(8)], reads=h2k + [("Wfi", 0, jb)])
                    U, uk = PY[jb % 2][:, 0:512], ("PY", jb % 2)
                    mm_group(U[:, 0:T2], uk, [(Wfi[:, k, DFF + jb * 128:DFF + (jb + 1) * 128], h2T[:, k, :]) for k in range(8)], reads=h2k + [("Wfi", 1, jb)])
                    sgt = sg[jb % 2]
                    S.op("act", lambda e, G=G, sgt=sgt: e.activation(out=sgt[:], in_=G[:, 0:T2], func=AF.Silu), reads=[gk], writes=[("sg", jb % 2)])
                    S.op("dve", lambda e, U=U, sgt=sgt, jb=jb: e.tensor_tensor(out=actT[:, jb, :], in0=sgt[:], in1=U[:, 0:T2], op=ALU.mult),
                         reads=[("sg", jb % 2), uk], writes=[("actT", jb)])
                ak_ = [("actT", jb) for jb in range(NJB)]
                S.op("dve", lambda e: e.memset(st2[:, 8:32], 0.0), writes=["st2b"])
                for c in range(NS):
                    PYt, pyk = PY[c % 2], ("PY", c % 2)
                    for nh in range(2):
                        mm_group(PYt[:, nh * 512:(nh + 1) * 512], pyk, [(actT[:, jb, c * 128:(c + 1) * 128], Wfo[:, jb, nh * 512:(nh + 1) * 512]) for jb in range(NJB)],
                                 reads=ak_ + [("Wfo", jb) for jb in range(NJB)])
                    for nh in range(2):
                        S.op("act", lambda e, PYt=PYt, c=c, nh=nh: e.activation(out=junk2[:, 0:512], in_=PYt[:, nh * 512:(nh + 1) * 512], func=AF.Square,
                                                                               accum_out=st2[:, 16 + 2 * c + nh:17 + 2 * c + nh]),
                             reads=[pyk, "st2b"], writes=["junk2", "st2b"])
                    S.op("dve", lambda e, c=c: e.tensor_tensor(out=st2[:, 12 + c:13 + c], in0=st2[:, 16 + 2 * c:17 + 2 * c], in1=st2[:, 17 + 2 * c:18 + 2 * c], op=ALU.add),
                         reads=["st2b"], writes=["st2b"])
                    S.op("dve", lambda e, c=c: e.tensor_scalar(out=st2[:, 12 + c:13 + c], in0=st2[:, 12 + c:13 + c], scalar1=1.0 / D, scalar2=EPS, op0=ALU.mult, op1=ALU.add),
                         reads=["st2b"], writes=["st2b"])
                    S.op("act", lambda e, c=c: e.activation(out=st2[:, 12 + c:13 + c], in_=st2[:, 12 + c:13 + c], func=AF.Ln), reads=["st2b"], writes=["st2b"])
                    S.op("act", lambda e, c=c: e.activation(out=st2[:, 12 + c:13 + c], in_=st2[:, 12 + c:13 + c], func=AF.Exp, scale=-0.5), reads=["st2b"], writes=["st2b"])
                    S.op("dve", lambda e, PYt=PYt, c=c: e.scalar_tensor_tensor(out=r2[:], in0=PYt[:, :], scalar=st2[:, 12 + c:13 + c], in1=gains2[:, 1, :], op0=ALU.mult, op1=ALU.mult),
                         reads=[pyk, "st2b", "gains2"], writes=["r2"])
                    S.op("dve", lambda e, c=c: e.tensor_tensor(out=x1[:, c, :], in0=x1[:, c, :], in1=r2[:], op=ALU.add), reads=[("x1", c), "r2"], writes=[("x1", c)])
                    r0 = (ti * NS + c) * 128
                    S.dma("sp", out[r0:r0 + 128, :], x1[:, c, :], reads=[("x1", c)], writes=[("out", r0)])
            S.barrier()
            S.emit()
    return nc


_CACHE = {}
_DBG = {"n_pre": NT1, "n_main": NT1, "p2": NT2}


def _consts(half):
    f32 = np.float32
    inv_freq = (10000.0 ** (-np.arange(0, 64, 2, dtype=f32) / f32(64))).astype(f32)
    pos = (half * HALF - 128 + np.arange((NCH + 1) * 128)).astype(f32)
    ang = (pos[:, None] * inv_freq[None, :]).astype(f32)
    emb = np.concatenate([ang, ang], axis=-1)
    cos = np.cos(emb).astype(f32)
    sin = np.sin(emb).astype(f32)
    sgn = np.concatenate([-np.ones(32, f32), np.ones(32, f32)])
    sinS = sin * sgn[None, :]
    cosT = np.ascontiguousarray(cos[:, :32].reshape(NCH + 1, 128, 32).transpose(1, 0, 2))
    sinT = np.ascontiguousarray(sinS.reshape(NCH + 1, 128, 64).transpose(1, 0, 2))
    q = np.arange(128)[:, None]
    kj = np.arange(256)[None, :]
    rel = q + 128 - kj
    band = ((rel >= 0) & (rel < 128)).astype(f32)
    first = band.copy()
    if half == 0:
        first[:, :128] = 0.0
    mask = np.stack([band, first], axis=1).astype(ml_dtypes.bfloat16)
    tri = (np.arange(128)[:, None] <= np.arange(128)[None, :]).astype(f32)
    return dict(cosT=cosT, sinS=sinT, mask=mask, idb=np.eye(128).astype(ml_dtypes.bfloat16), idf=np.eye(128, dtype=f32),
                trif=tri, trib=tri.astype(ml_dtypes.bfloat16), onesf=np.ones((128, 128), f32),
                flag=np.full((128, 1), float(half), f32))


def kernel(x, norm_pre_mix, norm_post_mix, norm_pre_ffn, norm_post_ffn, w_in, attn_sinks, conv_w, conv_b,
           b_igate, b_fgate, mlstm_head_norm, w_attn_branch, w_mlstm_branch, w_out, w_ffn_in, w_ffn_out):
    f32 = np.float32
    x = np.asarray(x, f32)
    rb = lambda v, n: np.ascontiguousarray(np.broadcast_to(np.asarray(v, f32).reshape(1, n), (128, n)))
    gains = np.ascontiguousarray(np.stack([rb(norm_pre_mix[0], D), rb(norm_post_mix[0], D), rb(norm_pre_ffn[0], D), rb(norm_post_ffn[0], D)], axis=1))
    convw = np.ascontiguousarray(np.asarray(conv_w[0], f32).reshape(4, 4, 128).transpose(2, 1, 0))
    convb = np.ascontiguousarray(np.asarray(conv_b[0], f32).reshape(4, 128).T)
    shared = dict(
        w_in=np.ascontiguousarray(w_in[0], f32), w_ab=np.ascontiguousarray(w_attn_branch[0], f32), w_mb=np.ascontiguousarray(w_mlstm_branch[0], f32),
        w_o=np.ascontiguousarray(w_out[0], f32), w_fi=np.ascontiguousarray(w_ffn_in[0], f32), w_fo=np.ascontiguousarray(w_ffn_out[0], f32),
        gains=gains, hng=rb(mlstm_head_norm[0], 512), sinks=rb(attn_sinks[0], 8),
        bif=rb(np.concatenate([np.asarray(b_igate[0], f32), np.asarray(b_fgate[0], f32)]), 8), convw=convw, convb=convb)
    zeros = np.zeros((HALF, D), f32)
    in_maps = []
    for c in range(NCORES):
        b, half = c // 2, c % 2
        m = dict(shared)
        m["xm"] = np.ascontiguousarray(x[b, half * HALF:(half + 1) * HALF])
        m["xp"] = np.ascontiguousarray(x[b, 0:HALF]) if half == 1 else zeros
        m.update(_consts(half))
        in_maps.append(m)
    if _CACHE.get("maps_only"):
        return in_maps
    if "nc" not in _CACHE:
        _CACHE["nc"] = build_program()
    res = run_bass_kernel_spmd(_CACHE["nc"], in_maps, core_ids=list(range(NCORES)))
    outp = np.empty((4, 2 * HALF, D), f32)
    for c in range(NCORES):
        b, half = c // 2, c % 2
        outp[b, half * HALF:(half + 1) * HALF] = res.results[c]["out"]
    return outp
```
